# Optimizing a Trainium2 kernel written in Bass

```python
import jax, jax.numpy as jnp
from jax import lax
import numpy as np

D_MODEL = 1024
BATCH = 32
SEQ = 2048
DEPTH = 2

GRID_W = 64
CTX_LEN = 256
CHUNK = 128
H_RET = 8
DK_RET = 128
DV_RET = 128
D_RET = H_RET * DV_RET
H_ML = 8
DH_ML = 128
D_ML = H_ML * DH_ML
CONV_W = 5
ROPE_BASE = 10000.0
EPS = 1e-6
NEG = -1e30
SPLITS = (H_RET * DK_RET, H_RET * DK_RET, D_RET, D_RET, D_ML, D_ML, D_ML, D_ML, D_ML, 4 * H_ML, D_MODEL, D_MODEL)
D_IN = sum(SPLITS)
GATE_OFF = 2 * H_RET * DK_RET + 2 * D_RET + 5 * D_ML

kernel_name = 'hybrid_retention_mlstm_dit'

F32 = jnp.float32


def _rmsnorm(x, w):
    x32 = x.astype(F32)
    y = x32 * lax.rsqrt(jnp.mean(x32 * x32, axis=-1, keepdims=True) + EPS)
    return (y * w.astype(F32)).astype(x.dtype)


def _head_norm(h, w):
    b, nh, l, d = h.shape
    h = h.astype(F32)
    hc = h - jnp.mean(h, axis=-1, keepdims=True)
    var = jnp.mean(hc * hc, axis=-1, keepdims=True)
    y = (hc * lax.rsqrt(var + EPS)).transpose(0, 2, 1, 3).reshape(b, l, nh * d)
    return y * w.astype(F32)


def _to_heads(t, nh):
    b, l, _ = t.shape
    return t.reshape(b, l, nh, -1).transpose(0, 2, 1, 3)


def _chunk(t):
    b, nh, l = t.shape[:3]
    return t.reshape(b, nh, l // CHUNK, CHUNK, *t.shape[3:])


def _flip(t):
    return jnp.flip(t, axis=2)


def _ident(t):
    return t


def _axial_rope_tables(l, d):
    rows_n = l // GRID_W
    rows = jnp.repeat(jnp.arange(rows_n, dtype=F32), GRID_W)
    cols = jnp.tile(jnp.arange(GRID_W, dtype=F32), rows_n)
    nf = d // 4
    freqs = ROPE_BASE ** (-jnp.arange(nf, dtype=F32) / nf)
    ang = jnp.concatenate([rows[:, None] * freqs, cols[:, None] * freqs], axis=-1)
    return jnp.cos(ang), jnp.sin(ang)


def _rope(t, cos, sin):
    half = t.shape[-1] // 2
    t1, t2 = t[..., :half], t[..., half:]
    return jnp.concatenate([t1 * cos - t2 * sin, t1 * sin + t2 * cos], axis=-1)


def _dwconv(t, w, b):
    ch = t.shape[-1]
    y = lax.conv_general_dilated(t, w[:, None, :], window_strides=(1,),
                                 padding=[(CONV_W // 2, CONV_W // 2)],
                                 dimension_numbers=('NWC', 'WIO', 'NWC'),
                                 feature_group_count=ch)
    return y + b


def _project(h, w_in, b_in, conv_w, conv_b, rope):
    u = jnp.einsum('bld,de->ble', h, w_in) + b_in
    idx = np.cumsum(SPLITS)[:-1].tolist()
    rq, rk, rv, rz, mq, mk, mv, mo, mz, mg, gr, gm = jnp.split(u, idx, axis=-1)
    mq, mk = jnp.split(jax.nn.silu(_dwconv(jnp.concatenate([mq, mk], axis=-1), conv_w, conv_b)), 2, axis=-1)
    rq, rk = _to_heads(rq, H_RET), _to_heads(rk, H_RET)
    if rope is not None:
        rq, rk = _rope(rq, *rope), _rope(rk, *rope)
    g = mg.astype(F32).transpose(0, 2, 1)
    ig_f, fg_f, ig_b, fg_b = jnp.split(g, 4, axis=1)
    return dict(rq=rq * DK_RET ** -0.5, rk=rk, rv=_to_heads(rv, H_RET), rz=rz,
                mq=_to_heads(mq, H_ML) * DH_ML ** -0.5, mk=_to_heads(mk, H_ML), mv=_to_heads(mv, H_ML),
                mo=mo, mz=mz, ig=(ig_f, ig_b),
                lf=(jax.nn.log_sigmoid(fg_f), jax.nn.log_sigmoid(fg_b)), gr=gr, gm=gm)


def _ret_states(k, v, lg, s0):
    kc, vc = _chunk(k.astype(F32)), _chunk(v.astype(F32))
    pos = jnp.arange(CHUNK, dtype=F32)
    lg = lg.astype(F32)
    zeta = jnp.exp(lg[:, None] * (CHUNK - 1.0 - pos))
    kv = jnp.einsum('bhncd,hc,bhnce->bhnde', kc, zeta, vc)
    decay = jnp.exp(lg * CHUNK)[:, None, None]

    def step(s, kv_n):
        return decay * s + kv_n, s

    s_fin, s_prev = lax.scan(step, s0, jnp.moveaxis(kv, 2, 0))
    return jnp.moveaxis(s_prev, 0, 2), s_fin


def _ret_outputs(q, k, v, lg, s_prev):
    b, nh, l, _ = q.shape
    qc, kc, vc = _chunk(q.astype(F32)), _chunk(k.astype(F32)), _chunk(v.astype(F32))
    pos = jnp.arange(CHUNK, dtype=F32)
    diff = pos[:, None] - pos[None, :]
    lg = lg.astype(F32)
    dmat = jnp.where(diff >= 0, jnp.exp(lg[:, None, None] * jnp.maximum(diff, 0.0)), 0.0)
    scores = jnp.einsum('bhnqd,bhnkd->bhnqk', qc, kc) * dmat[:, None]
    inner = jnp.einsum('bhnqk,bhnke->bhnqe', scores, vc)
    xi = jnp.exp(lg[:, None] * (pos + 1.0))
    cross = jnp.einsum('bhnqd,bhnde->bhnqe', qc, s_prev) * xi[:, None, :, None]
    return (inner + cross).reshape(b, nh, l, -1)


def _retention(pc, px, log_gamma, need_ctx):
    b = px['rk'].shape[0]
    s0 = jnp.zeros((b, H_RET, DK_RET, DV_RET), F32)
    out_x, out_c = [], []
    for d in range(2):
        fl = _flip if d else _ident
        lg = log_gamma[d]
        kc, vc = fl(pc['rk']), fl(pc['rv'])
        kx, vx = fl(px['rk']), fl(px['rv'])
        sp_c, sf_c = _ret_states(kc, vc, lg, s0)
        sp_x, _ = _ret_states(kx, vx, lg, sf_c)
        out_x.append(fl(_ret_outputs(fl(px['rq']), kx, vx, lg, sp_x)))
        if need_ctx:
            out_c.append(fl(_ret_outputs(fl(pc['rq']), kc, vc, lg, sp_c)))
    return out_x[0] + out_x[1], (out_c[0] + out_c[1] if need_ctx else None)


def _ml_states(k, v, ig, lf, st0):
    kc, vc = _chunk(k.astype(F32)), _chunk(v.astype(F32))
    igc, lfc = _chunk(ig), _chunk(lf)
    bcum = jnp.cumsum(lfc, axis=-1)
    g = bcum[..., -1]
    w = g[..., None] - bcum + igc
    a = jnp.max(w, axis=-1)
    ew = jnp.exp(w - a[..., None])
    kv = jnp.einsum('bhnc,bhncd,bhnce->bhnde', ew, kc, vc)
    ks = jnp.einsum('bhnc,bhncd->bhnd', ew, kc)

    def step(carry, inp):
        cm, nv, m = carry
        kv_n, ks_n, g_n, a_n = inp
        m_new = jnp.maximum(g_n + m, a_n)
        dec = jnp.exp(g_n + m - m_new)
        sc = jnp.exp(a_n - m_new)
        new = (dec[..., None, None] * cm + sc[..., None, None] * kv_n,
               dec[..., None] * nv + sc[..., None] * ks_n, m_new)
        return new, carry

    xs = tuple(jnp.moveaxis(t, 2, 0) for t in (kv, ks, g, a))
    st_fin, st_prev = lax.scan(step, st0, xs)
    return tuple(jnp.moveaxis(t, 0, 2) for t in st_prev), st_fin


def _ml_outputs(q, k, v, ig, lf, st_prev):
    b, nh, l, _ = q.shape
    qc, kc, vc = _chunk(q.astype(F32)), _chunk(k.astype(F32)), _chunk(v.astype(F32))
    igc, lfc = _chunk(ig), _chunk(lf)
    cp, npv, mp = st_prev
    bcum = jnp.cumsum(lfc, axis=-1)
    tri = jnp.tril(jnp.ones((CHUNK, CHUNK), dtype=bool))
    logd = jnp.where(tri, bcum[..., :, None] - bcum[..., None, :] + igc[..., None, :], NEG)
    m_inter = bcum + mp[..., None]
    m = jnp.maximum(jnp.max(logd, axis=-1), m_inter)
    s = jnp.einsum('bhnqd,bhnkd->bhnqk', qc, kc) * jnp.exp(logd - m[..., None])
    sc = jnp.exp(m_inter - m)
    num = jnp.einsum('bhnqk,bhnke->bhnqe', s, vc) + sc[..., None] * jnp.einsum('bhnqd,bhnde->bhnqe', qc, cp)
    den = jnp.sum(s, axis=-1) + sc * jnp.einsum('bhnqd,bhnd->bhnq', qc, npv)
    den = jnp.maximum(jnp.abs(den), jnp.exp(-m))
    return (num / den[..., None]).reshape(b, nh, l, -1)


def _mlstm(pc, px, need_ctx):
    b = px['mk'].shape[0]
    st0 = (jnp.zeros((b, H_ML, DH_ML, DH_ML), F32), jnp.zeros((b, H_ML, DH_ML), F32), jnp.zeros((b, H_ML), F32))
    out_x, out_c = [], []
    for d in range(2):
        fl = _flip if d else _ident
        args_c = (fl(pc['mk']), fl(pc['mv']), fl(pc['ig'][d]), fl(pc['lf'][d]))
        args_x = (fl(px['mk']), fl(px['mv']), fl(px['ig'][d]), fl(px['lf'][d]))
        sp_c, sf_c = _ml_states(*args_c, st0)
        sp_x, _ = _ml_states(*args_x, sf_c)
        out_x.append(fl(_ml_outputs(fl(px['mq']), *args_x, sp_x)))
        if need_ctx:
            out_c.append(fl(_ml_outputs(fl(pc['mq']), *args_c, sp_c)))
    return out_x[0] + out_x[1], (out_c[0] + out_c[1] if need_ctx else None)


def _merge(p, ret_h, ml_h, ret_norm_w, ml_norm_w, w_ret_o, w_ml_o, w_out):
    dt = w_out.dtype
    yr = _head_norm(ret_h, ret_norm_w) * jax.nn.silu(p['rz'].astype(F32))
    o = _to_heads(jax.nn.sigmoid(p['mo'].astype(F32)), H_ML)
    ym = _head_norm(o * ml_h, ml_norm_w) * jax.nn.silu(p['mz'].astype(F32))
    br = jnp.einsum('ble,ed->bld', yr.astype(dt), w_ret_o)
    bm = jnp.einsum('ble,ed->bld', ym.astype(dt), w_ml_o)
    y = jax.nn.sigmoid(p['gr']) * br + jax.nn.sigmoid(p['gm']) * bm
    return jnp.einsum('bld,de->ble', y, w_out)


def _layer(x, ctx, mod_x, mod_c, norm_w, w_in, b_in, conv_w, conv_b, log_gamma,
           ret_norm_w, ml_norm_w, w_ret_o, w_ml_o, w_out, rope, need_ctx):
    sh_x, sc_x, g_x = jnp.split(mod_x, 3, axis=-1)
    sh_c, sc_c, g_c = jnp.split(mod_c, 3, axis=-1)
    hx = _rmsnorm(x, norm_w) * (1.0 + sc_x[:, None]) + sh_x[:, None]
    hc = _rmsnorm(ctx, norm_w) * (1.0 + sc_c) + sh_c
    px = _project(hx, w_in, b_in, conv_w, conv_b, rope)
    pc = _project(hc, w_in, b_in, conv_w, conv_b, None)
    rx, rc = _retention(pc, px, log_gamma, need_ctx)
    mx, mc = _mlstm(pc, px, need_ctx)
    x = x + (g_x[:, None] * _merge(px, rx, mx, ret_norm_w, ml_norm_w, w_ret_o, w_ml_o, w_out)).astype(x.dtype)
    if need_ctx:
        ctx = ctx + (g_c * _merge(pc, rc, mc, ret_norm_w, ml_norm_w, w_ret_o, w_ml_o, w_out)).astype(ctx.dtype)
    return x, ctx


def setup_inputs(seed: int = 0) -> dict:
    key = jax.random.key(seed)
    ks = jax.random.split(key, 18)

    def nrm(k, shape, scale):
        return scale * jax.random.normal(k, shape, F32)

    x = nrm(ks[0], (BATCH, SEQ, D_MODEL), 1.0)
    c = nrm(ks[1], (BATCH, D_MODEL), 1.0)
    ctx = nrm(ks[2], (BATCH, CTX_LEN, D_MODEL), 1.0)
    c_ctx = nrm(ks[3], (D_MODEL,), 1.0)
    norm_w = 1.0 + nrm(ks[4], (DEPTH, D_MODEL), 0.02)
    w_ada = nrm(ks[5], (DEPTH, D_MODEL, 3 * D_MODEL), 0.5 * D_MODEL ** -0.5)
    b_ada = nrm(ks[6], (DEPTH, 3 * D_MODEL), 0.01)
    w_in = nrm(ks[7], (DEPTH, D_MODEL, D_IN), D_MODEL ** -0.5)
    f_bias = jnp.linspace(3.0, 6.0, H_ML, dtype=F32)
    z = jnp.zeros((H_ML,), F32)
    gate_bias = jnp.concatenate([z, f_bias, z, f_bias])
    b_in = nrm(ks[8], (DEPTH, D_IN), 0.01).at[:, GATE_OFF:GATE_OFF + 4 * H_ML].add(gate_bias)
    conv_w = nrm(ks[9], (DEPTH, CONV_W, 2 * D_ML), CONV_W ** -0.5)
    conv_b = nrm(ks[10], (DEPTH, 2 * D_ML), 0.01)
    base = jnp.log1p(-(2.0 ** (-5.0 - jnp.arange(H_RET, dtype=F32))))
    ret_log_gamma = base * (1.0 + nrm(ks[11], (DEPTH, 2, H_RET), 0.05))
    ret_norm_w = 1.0 + nrm(ks[12], (DEPTH, D_RET), 0.02)
    ml_norm_w = 1.0 + nrm(ks[13], (DEPTH, D_ML), 0.02)
    w_ret_o = nrm(ks[14], (DEPTH, D_RET, D_MODEL), D_RET ** -0.5)
    w_ml_o = nrm(ks[15], (DEPTH, D_ML, D_MODEL), D_ML ** -0.5)
    w_out = nrm(ks[16], (DEPTH, D_MODEL, D_MODEL), D_MODEL ** -0.5)
    final_norm_w = 1.0 + nrm(ks[17], (D_MODEL,), 0.02)
    return {'x': x, 'c': c, 'ctx': ctx, 'c_ctx': c_ctx, 'norm_w': norm_w, 'w_ada': w_ada, 'b_ada': b_ada,
            'w_in': w_in, 'b_in': b_in, 'conv_w': conv_w, 'conv_b': conv_b, 'ret_log_gamma': ret_log_gamma,
            'ret_norm_w': ret_norm_w, 'ml_norm_w': ml_norm_w, 'w_ret_o': w_ret_o, 'w_ml_o': w_ml_o,
            'w_out': w_out, 'final_norm_w': final_norm_w}


def reference(x, c, ctx, c_ctx, norm_w, w_ada, b_ada, w_in, b_in, conv_w, conv_b, ret_log_gamma,
              ret_norm_w, ml_norm_w, w_ret_o, w_ml_o, w_out, final_norm_w):
    rope = _axial_rope_tables(x.shape[1], DK_RET)
    for l in range(DEPTH):
        need_ctx = l < DEPTH - 1
        mod_x = jax.nn.silu(c) @ w_ada[l] + b_ada[l]
        mod_c = jax.nn.silu(c_ctx) @ w_ada[l] + b_ada[l]
        x, ctx = _layer(x, ctx, mod_x, mod_c, norm_w[l], w_in[l], b_in[l], conv_w[l], conv_b[l],
                        ret_log_gamma[l], ret_norm_w[l], ml_norm_w[l], w_ret_o[l], w_ml_o[l], w_out[l],
                        rope, need_ctx)
    return _rmsnorm(x, final_norm_w)
```

```python
import contextlib
import math
import numpy as np
import concourse.bass as bass
import concourse.mybir as mybir
from concourse.bass_utils import run_bass_kernel_spmd

F32 = mybir.dt.float32
BF16 = mybir.dt.bfloat16
I32 = mybir.dt.int32
AF = mybir.ActivationFunctionType
ALU = mybir.AluOpType
AX = mybir.AxisListType

N_DMA_SEMS = 24
D = 1024
DIN = 11296
GATE_OFF = 9216
EPS = 1e-6
SW = 2336
NRING = 6
CSCALE = 128.0 ** -0.5


class V:
    __slots__ = ("ap", "keys")

    def __init__(self, ap, keys):
        self.ap = ap
        self.keys = tuple(keys)


class Op:
    __slots__ = ("eng", "fn", "deps", "is_dma", "tick", "needed", "dsem", "dval", "idx")

    def __init__(self, eng, fn, is_dma, idx):
        self.eng = eng
        self.fn = fn
        self.deps = None
        self.is_dma = is_dma
        self.tick = None
        self.needed = False
        self.dsem = None
        self.dval = None
        self.idx = idx


class Prog:
    ENGS = ("pe", "act", "dve", "pool", "sp")

    def __init__(self, nc):
        self.nc = nc
        self.ops = []
        self.last_w = {}
        self.readers = {}
        self.ndma = 0
        self.dma_ops = []
        self.defer = None
        self.refseq = 0

    def collect(self, gen):
        assert self.defer is None
        self.defer = []
        for _ in gen:
            pass
        lst = self.defer
        self.defer = None
        return lst

    def schedule(self, streams, lat=0.3):
        from collections import deque
        ns = len(streams)
        pend_w = [dict() for _ in range(ns)]
        pend_a = [dict() for _ in range(ns)]
        for si, ops in enumerate(streams):
            for op in ops:
                rs = op[6]
                for v in op[2]:
                    for k in v.keys:
                        pend_a[si].setdefault(k, deque()).append(rs)
                for v in op[3]:
                    for k in v.keys:
                        pend_a[si].setdefault(k, deque()).append(rs)
                        pend_w[si].setdefault(k, deque()).append(rs)
        eng_free = {}
        wend, rend = {}, {}
        idx = [0] * ns
        total = sum(len(x) for x in streams)
        for _ in range(total):
            best = None
            for si, ops in enumerate(streams):
                i = idx[si]
                if i >= len(ops):
                    continue
                eng, fn, reads, writes, is_dma, cost, rs = ops[i]
                blocked = False
                if ns > 1:
                    for ti in range(ns):
                        if ti == si:
                            continue
                        pw, pa = pend_w[ti], pend_a[ti]
                        for v in reads:
                            for k in v.keys:
                                d = pw.get(k)
                                if d and d[0] < rs:
                                    blocked = True
                                    break
                            if blocked:
                                break
                        if blocked:
                            break
                        for v in writes:
                            for k in v.keys:
                                d = pa.get(k)
                                if d and d[0] < rs:
                                    blocked = True
                                    break
                            if blocked:
                                break
                        if blocked:
                            break
                if blocked:
                    continue
                ready = 0.0
                for v in reads:
                    for k in v.keys:
                        t = wend.get(k)
                        if t is not None and t > ready:
                            ready = t
                for v in writes:
                    for k in v.keys:
                        t = wend.get(k)
                        if t is not None and t > ready:
                            ready = t
                        t = rend.get(k)
                        if t is not None and t > ready:
                            ready = t
                start = max(ready + lat, eng_free.get(eng, 0.0))
                if best is None or start < best[0]:
                    best = (start, si)
            assert best is not None, "scheduler deadlock"
            start, si = best
            op = streams[si][idx[si]]
            idx[si] += 1
            eng, fn, reads, writes, is_dma, cost, rs = op
            for v in reads:
                for k in v.keys:
                    pend_a[si][k].popleft()
            for v in writes:
                for k in v.keys:
                    pend_a[si][k].popleft()
                    pend_w[si][k].popleft()
            if is_dma:
                eng_free[eng] = start + (0.6 if eng == "pool" else 0.15)
                end = start + cost
            else:
                end = start + cost
                eng_free[eng] = end
            for v in reads:
                for k in v.keys:
                    if rend.get(k, 0.0) < end:
                        rend[k] = end
            for v in writes:
                for k in v.keys:
                    wend[k] = end
            self.add(eng, fn, reads, writes, is_dma, cost)

    @staticmethod
    def _n(ap):
        n = 1
        for d in ap.shape[1:]:
            n *= d
        return n

    def add(self, eng, fn, reads, writes, is_dma=False, cost=0.3):
        if self.defer is not None:
            self.refseq += 1
            self.defer.append((eng, fn, reads, writes, is_dma, cost, self.refseq))
            return None
        idx = len(self.ops)
        deps = set()
        for v in reads:
            for k in v.keys:
                w = self.last_w.get(k)
                if w is not None:
                    deps.add(w)
        for v in writes:
            for k in v.keys:
                w = self.last_w.get(k)
                if w is not None:
                    deps.add(w)
                r = self.readers.get(k)
                if r:
                    deps.update(r.values())
        op = Op(eng, fn, is_dma, idx)
        if is_dma:
            j = self.ndma
            self.ndma += 1
            op.dsem = j % N_DMA_SEMS
            op.dval = 16 * (j // N_DMA_SEMS + 1)
            if j >= N_DMA_SEMS:
                deps.add(self.dma_ops[j - N_DMA_SEMS])
            self.dma_ops.append(idx)
        if eng == "pe" and not is_dma:
            deps = {d for d in deps if not (self.ops[d].eng == "pe" and not self.ops[d].is_dma)}
        op.deps = deps
        for d in deps:
            self.ops[d].needed = True
        for v in reads:
            for k in v.keys:
                rk = ("dma", idx) if is_dma else eng
                self.readers.setdefault(k, {})[rk] = idx
        for v in writes:
            for k in v.keys:
                self.last_w[k] = idx
                self.readers[k] = {}
        self.ops.append(op)
        return op

    def emit(self, final_dma_ops=()):
        nc = self.nc
        ops = self.ops
        cnt = {e: 0 for e in self.ENGS}
        for op in ops:
            if op.needed and not op.is_dma:
                cnt[op.eng] += 1
                op.tick = cnt[op.eng]
        by_eng = {e: [] for e in self.ENGS}
        for op in ops:
            by_eng[op.eng].append(op)
        with contextlib.ExitStack() as st:
            esem = {e: st.enter_context(nc.semaphore("s_" + e)) for e in self.ENGS}
            dsem = [st.enter_context(nc.semaphore("d%d" % i)) for i in range(N_DMA_SEMS)]
            block = st.enter_context(nc.Block())

            def run(engname, eng):
                waited = {}
                for op in by_eng[engname]:
                    for d in sorted(op.deps):
                        dop = ops[d]
                        if dop.is_dma:
                            key = ("d", dop.dsem)
                            sem, val = dsem[dop.dsem], dop.dval
                        else:
                            key = ("e", dop.eng)
                            sem, val = esem[dop.eng], dop.tick
                        if waited.get(key, 0) >= val:
                            continue
                        eng.wait_ge(sem, val)
                        waited[key] = val
                    ins = op.fn(eng)
                    if op.is_dma:
                        ins.then_inc(dsem[op.dsem], 16)
                    elif op.needed:
                        ins.then_inc(esem[op.eng], 1)
                if engname == "sp":
                    fin = {}
                    for o_ in ops:
                        if o_.is_dma:
                            fin[o_.dsem] = max(fin.get(o_.dsem, 0), o_.dval)
                    for ds_, dv_ in sorted(fin.items()):
                        eng.wait_ge(dsem[ds_], dv_)

            @block.tensor
            def _(e):
                run("pe", e)

            @block.scalar
            def _(e):
                run("act", e)

            @block.vector
            def _(e):
                run("dve", e)

            @block.gpsimd
            def _(e):
                run("pool", e)

            @block.sync
            def _(e):
                run("sp", e)

    def dma(self, out, in_, eng="sp"):
        nbytes = self._n(out.ap) * 128 * 4
        return self.add(eng, lambda e: e.dma_start(out=out.ap, in_=in_.ap), [in_], [out], is_dma=True,
                        cost=3.0 + nbytes / 150e3)

    def matmul(self, out, lhsT, rhs, start=True, stop=True):
        return self.add("pe", lambda e: e.matmul(out.ap, lhsT.ap, rhs.ap, start=start, stop=stop),
                        [lhsT, rhs], [out], cost=max(0.11, self._n(rhs.ap) * 0.00047 + 0.02))

    def transpose(self, out, in_, ident):
        return self.add("pe", lambda e: e.transpose(out.ap, in_.ap, ident.ap), [in_, ident], [out], cost=0.11)

    def act(self, out, in_, func, bias=None, scale=None, accum_out=None):
        reads = [in_]
        kw = {}
        if isinstance(bias, V):
            reads.append(bias)
            kw["bias"] = bias.ap
        elif bias is not None:
            kw["bias"] = bias
        if isinstance(scale, V):
            reads.append(scale)
            kw["scale"] = scale.ap
        elif scale is not None:
            kw["scale"] = scale
        writes = [out]
        if accum_out is not None:
            writes.append(accum_out)
            kw["accum_out"] = accum_out.ap
        return self.add("act", lambda e: e.activation(out.ap, in_.ap, func, **kw), reads, writes,
                        cost=(self._n(out.ap) + 335) / 1200.0 + (1.3 if func == AF.Sqrt else 0.0))

    def tt(self, out, in0, in1, op, eng="dve"):
        return self.add(eng, lambda e: e.tensor_tensor(out.ap, in0.ap, in1.ap, op), [in0, in1], [out], cost=self._ec(eng, out))

    def ts(self, out, in0, s1, op0, s2=None, op1=None, eng="dve"):
        reads = [in0]
        a1 = s1
        if isinstance(s1, V):
            reads.append(s1)
            a1 = s1.ap
        a2 = s2
        if isinstance(s2, V):
            reads.append(s2)
            a2 = s2.ap
        if op1 is None:
            return self.add(eng, lambda e: e.tensor_single_scalar(out.ap, in0.ap, a1, op0), reads, [out], cost=self._ec(eng, out))
        return self.add(eng, lambda e: e.tensor_scalar(out.ap, in0.ap, a1, a2, op0, op1), reads, [out], cost=self._ec(eng, out))

    def stt(self, out, in0, scalar, in1, op0, op1, eng="dve"):
        reads = [in0, in1]
        a = scalar
        if isinstance(scalar, V):
            reads.append(scalar)
            a = scalar.ap
        return self.add(eng, lambda e: e.scalar_tensor_tensor(out.ap, in0.ap, a, in1.ap, op0, op1), reads, [out], cost=self._ec(eng, out))

    def copy(self, out, in_, eng="dve"):
        if eng == "act":
            return self.add("act", lambda e: e.copy(out.ap, in_.ap), [in_], [out], cost=(self._n(out.ap) + 335) / 1200.0)
        return self.add(eng, lambda e: e.tensor_copy(out.ap, in_.ap), [in_], [out], cost=self._ec(eng, out))

    def reduce(self, out, in_, op, eng="dve"):
        return self.add(eng, lambda e: e.tensor_reduce(out.ap, in_.ap, AX.X, op), [in_], [out], cost=(self._n(in_.ap) + 150) / 960.0)

    def memset(self, out, val, eng="dve"):
        return self.add(eng, lambda e: e.memset(out.ap, val), [], [out], cost=self._ec(eng, out))

    def recip(self, out, in_):
        return self.add("dve", lambda e: e.reciprocal(out.ap, in_.ap), [in_], [out], cost=self._ec("dve", out))

    def _ec(self, eng, out):
        n = self._n(out.ap)
        if eng == "pool":
            return (2.4 * n + 150) / 960.0
        return 1.9 * (n + 150) / 960.0


V_BADA = 0
V_NORMW = 24
V_BIN = 32
V_BG = 104
V_CONVB = 120
V_CONVW = 136
V_RNW = 216
V_MNW = 224
V_BSW = 232
NV = 248
C_ID, C_MF, C_MB, C_QMK, C_KMQ, C_POS = 0, 128, 256, 384, 512, 640
NCST = 648


def build_program(NSEQ, LC, LX, DEPTH, dbg=False):
    nc = bass.Bass("TRN2", target_bir_lowering=False)
    T = LC + LX
    NCH = T // 128
    NCC = LC // 128
    NS1 = NSEQ + 1
    LNC = math.log(CSCALE)

    def dram(name, shape, kind="ExternalInput", dt=F32):
        return nc.dram_tensor(name, list(shape), dt, kind=kind).ap()

    xin = dram("xin", [NSEQ, T, D])
    out = dram("out", [NSEQ, LX, D], kind="ExternalOutput")
    xs = dram("xs", [NSEQ, T, D], kind="Internal")
    ys = dram("ys", [16, 128, T], kind="Internal", dt=BF16)
    cT = dram("cT", [128, 8 * NS1])
    w_ada = dram("w_ada", [DEPTH, D, 3 * D])
    w_in = dram("w_in", [DEPTH, D, DIN])
    w_ro = dram("w_ret_o", [DEPTH, D, D])
    w_mo = dram("w_ml_o", [DEPTH, D, D])
    w_out = dram("w_out", [DEPTH, D, D])
    vtt = dram("vtt", [DEPTH, 128, NV])
    bgr = dram("bgr", [DEPTH, 128, 32])
    lgb_d = dram("lgb", [128, DEPTH * 16])
    cst_d = dram("cst", [128, NCST])
    rope_d = dram("rope", [2, 128, LX])
    fnw_d = dram("fnw", [128, D])
    dbg_outs = {}

    blocks = []
    for i in range(0, LC, 512):
        blocks.append((i, min(512, LC - i), False))
    for i in range(0, LX, 512):
        blocks.append((LC + i, min(512, LX - i), True))
    fwd_order = list(range(NCH))
    bwd_order = list(reversed(range(NCC))) + list(reversed(range(NCC, NCH)))

    st = contextlib.ExitStack()
    with st:
        def sb(name, shape, dt=F32):
            return st.enter_context(nc.sbuf_tensor(name, list(shape), dt))

        p = Prog(nc)
        NSLOT = 32
        ARENA = sb("arena", [128, NSLOT * SW], BF16)
        RING = sb("ring", [128, NRING * 1024], BF16)
        ROPE = sb("rope_sb", [128, 2 * LX], F32)
        CST = sb("cst_sb", [128, NCST], F32)
        IDB = sb("idb", [128, 128], BF16)
        PERMB = sb("permb", [128, 128], BF16)
        ABF = sb("abf", [128, 1024], BF16)
        MFC = sb("mfc", [128, 256], F32)
        LGB = sb("lgb_sb", [128, DEPTH * 16], F32)
        VTT = sb("vtt_sb", [128, DEPTH * NV], F32)
        BGR = sb("bgr_sb", [128, DEPTH * 32], F32)
        SILC = sb("silc", [128, 8 * NS1], F32)
        MODF = sb("modf", [128, DEPTH * 24 * NS1], F32)
        AMOD = sb("amod", [128, DEPTH * 8 * NS1], F32)
        RC = sb("rc", [128, DEPTH * 48], F32)
        ONES = sb("ones", [128, 128], F32)
        EW = sb("ew", [128, NCH * 16], F32)
        THR = sb("thr", [128, NCH * 16], F32)
        ALB = sb("alb", [128, NCH * 16], F32)
        DM = sb("dm", [128, 128], F32)
        TMP = [sb("tmp%d" % i, [128, 512], F32) for i in range(4)]
        STB1 = sb("stb1", [128, 512], BF16)
        XO = sb("xo", [128, 1024], F32)
        SF = sb("sf", [128, 129], F32)
        SBK = sb("sbk", [128, 129], F32)
        SFB = [sb("sfb%d" % i, [128, 130], BF16) for i in range(2)]
        DG = sb("dg", [128, 10 * 128], BF16)
        SM = sb("small", [128, 112], F32)
        G8 = sb("g8", [8, 22 * NCH], F32)
        PS = [st.enter_context(nc.psum_tensor("ps%d" % i, [128, 512], F32)) for i in range(8)]

        def av(slot, c0, n, dt=BF16):
            esz = 2 if dt == BF16 else 4
            b0 = c0 * 2
            b1 = b0 + n * esz
            assert b1 <= SW * 2, (slot, c0, n)
            keys = [("A", slot, j) for j in range(b0 // 1024, (b1 - 1) // 1024 + 1)]
            base = slot * SW + c0
            if dt == BF16:
                ap = ARENA[:, base:base + n]
            else:
                assert base % 2 == 0
                ap = ARENA[:, base:base + 2 * n].bitcast(F32)
            return V(ap, keys)

        def av3(slot, c0, nb, w, dt=BF16, stride=None, sub=None):
            stride = stride or w
            v = av(slot, c0, nb * stride, dt)
            ap = v.ap.rearrange("p (a b) -> p a b", b=stride)
            if stride != w or sub is not None:
                lo, hi = (0, w) if sub is None else sub
                ap = ap[:, :, lo:hi]
            return V(ap, v.keys)

        def avk(slot0, nsl, c0, n):
            b0 = c0 * 2
            b1 = b0 + n * 2
            keys = [("A", slot0 + s, j) for s in range(nsl) for j in range(b0 // 1024, (b1 - 1) // 1024 + 1)]
            ap = ARENA[:, slot0 * SW:(slot0 + nsl) * SW].rearrange("p (k w) -> p k w", w=SW)[:, :, c0:c0 + n]
            return V(ap, keys)

        def pv(bank, c0=0, n=512, dt=F32, rows=128):
            if dt == F32:
                ap = PS[bank][0:rows, c0:c0 + n]
            else:
                ap = PS[bank][0:rows, :].bitcast(BF16)[:, c0:c0 + n]
            return V(ap, [("P", bank)])

        def pv3(bank, nb, w, dt=F32, stride=None, c0=0, rows=128, sub=None):
            stride = stride or w
            v = pv(bank, c0, nb * stride, dt, rows)
            ap = v.ap.rearrange("p (a b) -> p a b", b=stride)
            if stride != w or sub is not None:
                lo, hi = (0, w) if sub is None else sub
                ap = ap[:, :, lo:hi]
            return V(ap, v.keys)

        def tv(t, c0, n, key, rows=128):
            return V(t[0:rows, c0:c0 + n], [key])

        def tv3(t, c0, nb, w, key, rows=128):
            return V(t[0:rows, c0:c0 + nb * w].rearrange("p (a b) -> p a b", b=w), [key])

        def cstv(c0, n=128, rows=128):
            return V(CST[0:rows, c0:c0 + n], ["CST"])

        def vcol(l, c):
            return V(VTT[:, l * NV + c:l * NV + c + 1], ["VTT"])

        def bc3(v, shape):
            return V(v.ap.unsqueeze(2).to_broadcast(list(shape)), v.keys)

        def bc_mid(v, shape):
            return V(v.ap.unsqueeze(1).to_broadcast(list(shape)), v.keys)

        ring_i = [0]

        def ring_load(src_ap, ncols=128, width=None):
            r = ring_i[0]
            ring_i[0] = (r + 1) % NRING
            base = r * 1024
            dst = RING[:, base:base + 8 * ncols].rearrange("p (k n) -> p k n", n=ncols)
            v = V(dst, [("R", r)])
            p.dma(v, V(src_ap.rearrange("(k p) n -> p k n", p=128), ["wdram"]), eng="pool")
            return v, r

        def ring_slice(r, ncols, k, c0=0, n=None):
            n = ncols if n is None else n
            base = r * 1024 + k * ncols + c0
            return V(RING[:, base:base + n], [("R", r)])

        S_HT = 0
        S_YR = 8
        S_YM = 16
        S_W = 24

        def hT(k, t0, n):
            return av(S_HT + k, t0, n)

        IDBv = V(IDB[:], ["IDB"])
        IDFv = cstv(C_ID)

        p.dma(V(CST[:], ["CST"]), V(cst_d, ["d_cst"]))
        p.dma(V(ROPE[:].rearrange("p (a n) -> p a n", a=2), ["ROPE"]), V(rope_d.rearrange("a p n -> p a n"), ["d_rope"]))
        p.dma(V(LGB[:], ["LGB"]), V(lgb_d, ["d_lgb"]))
        p.dma(V(VTT[:].rearrange("p (l n) -> p l n", l=DEPTH), ["VTT"]), V(vtt.rearrange("l p n -> p l n"), ["d_vtt"]))
        p.dma(V(BGR[:].rearrange("p (l n) -> p l n", l=DEPTH), ["BGR"]), V(bgr.rearrange("l p n -> p l n"), ["d_bgr"]))
        p.dma(V(SILC[:], ["SILC"]), V(cT, ["d_cT"]))
        p.copy(IDBv, IDFv)
        p.copy(V(PERMB[:, 0:64], ["PERMB"]), cstv(C_ID + 64, 64))
        p.copy(V(PERMB[:, 64:128], ["PERMB"]), cstv(C_ID, 64))
        p.memset(V(ONES[:], ["ONES"]), 1.0)
        p.ts(V(MFC[:, 0:128], ["MFC"]), cstv(C_MF), CSCALE, ALU.mult)
        p.ts(V(MFC[:, 128:256], ["MFC"]), cstv(C_MB), CSCALE, ALU.mult)
        p.act(V(SILC[:], ["SILC"]), V(SILC[:], ["SILC"]), AF.Silu)

        for l in range(DEPTH):
            for kind, (cf, cb) in enumerate(((0, 1), (2, 3), (4, 4))):
                base = l * 48 + kind * 16
                p.ts(V(RC[:, base:base + 8], ["RC", l]), V(LGB[:, l * 16:l * 16 + 8], ["LGB"]),
                     V(CST[:, C_POS + cf:C_POS + cf + 1], ["CST"]), ALU.mult)
                p.ts(V(RC[:, base + 8:base + 16], ["RC", l]), V(LGB[:, l * 16 + 8:l * 16 + 16], ["LGB"]),
                     V(CST[:, C_POS + cb:C_POS + cb + 1], ["CST"]), ALU.mult)
            p.act(V(RC[:, l * 48:l * 48 + 48], ["RC", l]), V(RC[:, l * 48:l * 48 + 48], ["RC", l]), AF.Exp)
            p.ts(V(RC[:, l * 48:l * 48 + 16], ["RC", l]), V(RC[:, l * 48:l * 48 + 16], ["RC", l]), CSCALE, ALU.mult)
            for g in range(6):
                buf = (l * 6 + g) % 2
                s0 = (8 if buf == 0 else 19)
                stg_keys = [("A", s0 + s2, j) for s2 in range(8) for j in range(5)]
                stg = ARENA[:, s0 * SW:s0 * SW + 8192].bitcast(F32).rearrange("p (k n) -> p k n", n=512)
                p.dma(V(stg, stg_keys), V(w_ada[l][:, g * 512:(g + 1) * 512].rearrange("(k p) n -> p k n", p=128), ["wdram"]))
                for jj in range(4):
                    j = g * 4 + jj
                    for k in range(8):
                        p.matmul(pv(0, j * NS1, NS1), V(stg[:, k, jj * 128:(jj + 1) * 128], stg_keys),
                                 V(SILC[:, k * NS1:(k + 1) * NS1], ["SILC"]), start=(k == 0), stop=(k == 7))
            p.tt(tv3(MODF, l * 24 * NS1, 24, NS1, ("MODF", l)), pv3(0, 24, NS1),
                 bc3(V(VTT[:, l * NV + V_BADA:l * NV + V_BADA + 24], ["VTT"]), [128, 24, NS1]), ALU.add)
            p.ts(tv(AMOD, l * 8 * NS1, 8 * NS1, ("AMOD", l)), tv(MODF, (l * 24 + 8) * NS1, 8 * NS1, ("MODF", l)), 1.0, ALU.add)
            p.tt(tv3(AMOD, l * 8 * NS1, 8, NS1, ("AMOD", l)), tv3(AMOD, l * 8 * NS1, 8, NS1, ("AMOD", l)),
                 bc3(V(VTT[:, l * NV + V_NORMW:l * NV + V_NORMW + 8], ["VTT"]), [128, 8, NS1]), ALU.mult)

        def mod_sh(l, j, who):
            c = (l * 24 + j) * NS1 + who
            return V(MODF[:, c:c + 1], [("MODF", l)])

        def rsqrt_dve(y, x, t):
            yi = V(y.ap.bitcast(I32), y.keys)
            xi = V(x.ap.bitcast(I32), x.keys)
            p.add("dve", lambda e: e.tensor_single_scalar(yi.ap, xi.ap, 1, ALU.logical_shift_right), [x], [y], cost=0.2)
            p.add("dve", lambda e: e.tensor_scalar(yi.ap, yi.ap, -1.0, float(0x5f3759df), ALU.mult, ALU.add), [y], [y], cost=0.2)
            for _ in range(2):
                p.tt(t, x, y, ALU.mult)
                p.tt(t, t, y, ALU.mult)
                p.ts(t, t, -0.5, ALU.mult, 1.5, ALU.add)
                p.tt(y, y, t, ALU.mult)

        def rmsnorm_stats(xv, junk, ss, rstd, tmpc):
            p.memset(ss, 0.0)
            p.act(junk, xv, AF.Square, accum_out=ss)
            p.ts(ss, ss, 1.0 / D, ALU.mult, EPS, ALU.add)
            rsqrt_dve(rstd, ss, tmpc)

        out_dmas = []

        WS_BASE = 8
        WS_N = 11
        pb_i = [0]

        PBANKS = (0, 1, 2, 7)

        def pbank():
            b = pb_i[0]
            pb_i[0] = (b + 1) % 4
            return PBANKS[b]

        def fkeys(a, b):
            ks = []
            for slot in range(a // SW, (b - 1) // SW + 1):
                lo = max(a, slot * SW) - slot * SW
                hi = min(b, (slot + 1) * SW) - slot * SW
                ks += [("A", slot, j) for j in range((lo * 2) // 1024, (hi * 2 - 1) // 1024 + 1)]
            return ks

        def interleave(ga, gb):
            ta = tb = 0.0
            a_done = b_done = False
            while not (a_done and b_done):
                if not a_done and (b_done or ta <= tb):
                    try:
                        ta += next(ga) or 1.0
                    except StopIteration:
                        a_done = True
                elif not b_done:
                    try:
                        tb += next(gb) or 1.0
                    except StopIteration:
                        b_done = True

        def drain(g):
            for _ in g:
                pass

        yst_i = [0]
        rope_i = [0]

        for s in range(NSEQ):
            for l in range(DEPTH):
                last = (l == DEPTH - 1)
                src = xin if l == 0 else xs
                wl = w_in[l]
                smc = lambda c, n=1: V(SM[:, c:c + n], [("SM", c)])

                SSQ = V(SM[:, 44:44 + NCH], [("SM", 44)])
                p.memset(SSQ, 0.0)
                for i in range(NCH):
                    xst = av(WS_BASE + (i % 3), 0, 1024, F32)
                    junk = av(WS_BASE + 3, 0, 1024, F32)
                    p.dma(xst, V(src[s, i * 128:(i + 1) * 128, :], [("xs", s, i)]))
                    p.act(junk, xst, AF.Square, accum_out=V(SM[:, 44 + i:45 + i], [("SM", 44)]))
                p.ts(SSQ, SSQ, 1.0 / D, ALU.mult, EPS, ALU.add)
                RSTD = V(SM[:, 64:64 + NCH], [("SM", 64)])
                rsqrt_dve(RSTD, SSQ, V(SM[:, 84:84 + NCH], [("SM", 84)]))
                for i in range(NCH):
                    who = NSEQ if i < NCC else s
                    xst = av(WS_BASE + (i % 3), 0, 1024, F32)
                    xn = av(WS_BASE + 4 + (i % 2), 0, 1024)
                    bnk = i % 2
                    p.dma(xst, V(src[s, i * 128:(i + 1) * 128, :], [("xs", s, i)]))
                    p.act(xn, xst, AF.Copy, scale=V(SM[:, 64 + i:65 + i], [("SM", 64)]))
                    for j in range(8):
                        p.transpose(pv(bnk, j * 128, 128, BF16), V(xn.ap[:, j * 128:(j + 1) * 128], xn.keys), IDBv)
                    t3 = av3(WS_BASE + 6 + (i % 2), 0, 8, 128, F32)
                    a_bc = V(AMOD[:, l * 8 * NS1:(l + 1) * 8 * NS1].rearrange("p (j s) -> p j s", s=NS1)[:, :, who:who + 1]
                             .to_broadcast([128, 8, 128]), [("AMOD", l)])
                    b_bc = V(MODF[:, l * 24 * NS1:(l * 24 + 8) * NS1].rearrange("p (j s) -> p j s", s=NS1)[:, :, who:who + 1]
                             .to_broadcast([128, 8, 128]), [("MODF", l)])
                    p.tt(t3, pv3(bnk, 8, 128, BF16), a_bc, ALU.mult)
                    p.tt(avk(S_HT, 8, i * 128, 128), t3, b_bc, ALU.add)

                def gate_prep():
                    wg, rg = ring_load(wl[:, GATE_OFF:GATE_OFF + 32], ncols=32)
                    GTs, NLFs, BCs, BKs, MUBs = [WS_BASE + WS_N + i for i in range(5)]
                    for i in range(NCH):
                        bank = 3 if i < 16 else 4
                        ii = i % 16
                        for k in range(8):
                            p.matmul(pv(bank, ii * 32, 32), hT(k, i * 128, 128), ring_slice(rg, 32, k),
                                     start=(k == 0), stop=(k == 7))
                    for b0 in range(0, NCH, 16):
                        nb = min(16, NCH - b0)
                        p.tt(av3(GTs, 2 * b0 * 32, nb, 32, F32), pv3(3 + b0 // 16, nb, 32),
                             bc_mid(V(BGR[:, l * 32:(l + 1) * 32], ["BGR"]), [128, nb, 32]), ALU.add)
                    for d in range(2):
                        order = fwd_order if d == 0 else bwd_order
                        gt3 = av3(GTs, 0, NCH, 32, F32)
                        fcols = V(gt3.ap[:, :, 8 + 16 * d:16 + 16 * d], gt3.keys)
                        icols = V(gt3.ap[:, :, 16 * d:16 * d + 8], gt3.keys)
                        nlf = av3(NLFs, 0, NCH, 8, F32)
                        p.act(nlf, fcols, AF.Exp, scale=-1.0)
                        p.ts(nlf, nlf, 1.0, ALU.add)
                        p.act(nlf, nlf, AF.Ln)
                        p.matmul(pv(5, 0, NCH * 8), cstv(C_MF if d == 0 else C_MB), av(NLFs, 0, NCH * 8, F32))
                        bc = av3(BCs, 0, NCH, 8, F32)
                        p.copy(bc, pv3(5, NCH, 8))
                        bk = av3(BKs, 0, NCH, 8, F32)
                        p.tt(bk, icols, bc, ALU.add)
                        g8 = lambda c0, n, key: V(G8[0:8, c0:c0 + n], [("G8", key)])
                        BMAX, GSUM, MP, MU = 0, NCH, 2 * NCH, 3 * NCH
                        for c0 in range(0, NCH, 4):
                            nb = min(4, NCH - c0)
                            for i in range(nb):
                                p.transpose(pv(6, i * 128, 128, rows=8), V(bk.ap[:, c0 + i, :], bk.keys), IDFv)
                            p.reduce(g8(BMAX + c0, nb, "bmax"), pv3(6, nb, 128, rows=8), ALU.max)
                        for n_ in range(NCH):
                            p.matmul(pv(3, n_, 1, rows=8), V(nlf.ap[:, n_, :], nlf.keys), V(ONES[:, 0:1], ["ONES"]))
                        p.copy(g8(GSUM, NCH, "gsum"), pv(3, 0, NCH, rows=8))
                        p.memset(g8(MP + order[0], 1, ("mp", order[0])), 0.0)
                        for si, n_ in enumerate(order):
                            p.tt(g8(MU + n_, 1, ("mu", n_)), g8(MP + n_, 1, ("mp", n_)), g8(BMAX + n_, 1, "bmax"), ALU.max)
                            if si + 1 < NCH:
                                n2 = order[si + 1]
                                p.tt(g8(MP + n2, 1, ("mp", n2)), g8(MU + n_, 1, ("mu", n_)), g8(GSUM + n_, 1, "gsum"), ALU.subtract)
                        allk = [("G8", ("mp", n_)) for n_ in range(NCH)] + [("G8", ("mu", n_)) for n_ in range(NCH)]
                        MPv = V(G8[0:8, MP:MP + NCH], allk)
                        MUv = V(G8[0:8, MU:MU + NCH], allk)
                        AL0 = 4 * NCH
                        alv = g8(AL0, NCH, "al")
                        p.tt(alv, MPv, MUv, ALU.subtract)
                        p.act(alv, alv, AF.Exp)
                        BD0 = 5 * NCH
                        bd = V(G8[0:8, BD0:BD0 + 16 * NCH].rearrange("p (w h n) -> p w h n", w=2, h=8), [("G8", "bd")])
                        i8 = V(CST[0:8, C_ID:C_ID + 8].unsqueeze(2).to_broadcast([8, 8, NCH]), ["CST"])
                        p.tt(V(bd.ap[:, 0], bd.keys), V(MUv.ap.unsqueeze(1).to_broadcast([8, 8, NCH]), MUv.keys), i8, ALU.mult)
                        p.tt(V(bd.ap[:, 1], bd.keys), V(alv.ap.unsqueeze(1).to_broadcast([8, 8, NCH]), alv.keys), i8, ALU.mult)
                        p.matmul(pv(4, 0, 16 * NCH), V(ONES[0:8, :], ["ONES"]), V(G8[0:8, BD0:BD0 + 16 * NCH], [("G8", "bd")]))
                        mub = av3(MUBs, 0, NCH, 8, F32)
                        psb = PS[4][:, 0:16 * NCH].rearrange("p (w h n) -> p w n h", w=2, h=8)
                        p.copy(mub, V(psb[:, 0], [("P", 4)]))
                        p.copy(V(ALB[:].rearrange("p (n r) -> p n r", r=16)[:, :, 8 * d:8 * d + 8], ["ALB"]), V(psb[:, 1], [("P", 4)]))
                        ewv = V(EW[:].rearrange("p (n r) -> p n r", r=16)[:, :, 8 * d:8 * d + 8], ["EW"])
                        thv = V(THR[:].rearrange("p (n r) -> p n r", r=16)[:, :, 8 * d:8 * d + 8], ["THR"])
                        p.tt(ewv, bk, mub, ALU.subtract)
                        p.act(ewv, ewv, AF.Exp)
                        p.tt(thv, bc, mub, ALU.subtract)
                        p.ts(thv, thv, -LNC, ALU.add)
                        p.act(thv, thv, AF.Exp)
                    yield 1.0

                def proj(wv_r, t0, n, bank):
                    for k in range(8):
                        p.matmul(pv(bank, 0, n), ring_slice(wv_r, 128, k), hT(k, t0, n), start=(k == 0), stop=(k == 7))
                    return pv(bank, 0, n)

                def transposes_bf16(src_fn, nb, bank):
                    for i in range(nb):
                        p.transpose(pv(bank, i * 128, 128, BF16), src_fn(i), IDBv)

                def head_norm_and_store(rt, nb, n, ys_idx, t0, nw_col, sz_slot, ws, tbank, sq_slot, sq_c0, hn_slot, hn_c0):
                    rt3 = V(rt.ap.rearrange("p (a b) -> p a b", b=128), rt.keys)
                    sq = av(sq_slot, sq_c0, n, F32)
                    s1 = smc(8, nb)
                    s2 = smc(12, nb)
                    m = smc(16, nb)
                    m2 = smc(20, nb)
                    p.reduce(s1, rt3, ALU.add)
                    for i in range(nb):
                        p.act(V(sq.ap[:, i * 128:(i + 1) * 128], sq.keys), V(rt.ap[:, i * 128:(i + 1) * 128], rt.keys), AF.Square,
                              accum_out=V(SM[:, 12 + i:13 + i], [("SM", 12)]))
                    p.ts(m, s1, 1.0 / 128, ALU.mult)
                    p.tt(m2, m, m, ALU.mult)
                    p.stt(m2, s2, 1.0 / 128, m2, ALU.mult, ALU.subtract)
                    p.ts(s2, m2, EPS, ALU.add)
                    rsqrt_dve(m2, s2, smc(40, nb))
                    p.stt(m, m, -1.0, m2, ALU.mult, ALU.mult)
                    hn = av(hn_slot, hn_c0, n)
                    for i in range(nb):
                        p.act(V(hn.ap[:, i * 128:(i + 1) * 128], hn.keys), V(rt.ap[:, i * 128:(i + 1) * 128], rt.keys), AF.Identity,
                              scale=V(SM[:, 20 + i:21 + i], [("SM", 20)]), bias=V(SM[:, 16 + i:17 + i], [("SM", 16)]))
                    transposes_bf16(lambda i: V(hn.ap[:, i * 128:(i + 1) * 128], hn.keys), nb, tbank)
                    yb = yst_i[0]
                    yst_i[0] = 1 - yb
                    yst = av(30 + yb, 0, n)
                    p.stt(yst, pv(tbank, 0, n, BF16), nw_col, av(sz_slot, t0, n), ALU.mult, ALU.mult)
                    p.dma(V(ys[ys_idx, :, t0:t0 + n], [("ys", ys_idx, t0)]), yst)

                def P_ret(h, ws):
                    QT, KT, KZF, KZB, VTK, SZ, SBST = [ws + i for i in range(7)]
                    rcx = lambda kind, dd: V(RC[:, l * 48 + kind * 16 + dd * 8 + h:l * 48 + kind * 16 + dd * 8 + h + 1], [("RC", l)])
                    lgc = lambda dd: V(LGB[:, l * 16 + dd * 8 + h:l * 16 + dd * 8 + h + 1], ["LGB"])
                    p.act(V(TMP[0][:, 0:128], ["TMP0"]), cstv(C_QMK), AF.Exp, scale=lgc(0))
                    p.act(V(TMP[1][:, 0:128], ["TMP1"]), cstv(C_KMQ), AF.Exp, scale=lgc(1))
                    p.tt(V(TMP[0][:, 0:128], ["TMP0"]), V(TMP[0][:, 0:128], ["TMP0"]), V(MFC[:, 0:128], ["MFC"]), ALU.mult, eng="pool")
                    p.tt(V(TMP[1][:, 0:128], ["TMP1"]), V(TMP[1][:, 0:128], ["TMP1"]), V(MFC[:, 128:256], ["MFC"]), ALU.mult, eng="pool")
                    p.tt(V(DM[:], ["DM"]), V(TMP[0][:, 0:128], ["TMP0"]), V(TMP[1][:, 0:128], ["TMP1"]), ALU.add, eng="pool")
                    yield 0.5

                    def load_sw(c0):
                        r = ring_i[0]
                        ring_i[0] = (r + 1) % NRING
                        base = r * 1024
                        dst = RING[:, base:base + 1024].rearrange("p (k n) -> p k n", n=128)
                        for half in range(2):
                            p.dma(V(dst[:, :, half * 64:(half + 1) * 64], [("R", r)]),
                                  V(wl[:, c0 + (1 - half) * 64:c0 + (1 - half) * 64 + 64].rearrange("(k p) n -> p k n", p=128), ["wdram"]),
                                  eng="pool")
                        return r

                    _, rw_k = ring_load(wl[:, 1024 + h * 128:1024 + h * 128 + 128])
                    _, rv = ring_load(wl[:, 2048 + h * 128:2048 + h * 128 + 128])
                    _, rw_q = ring_load(wl[:, h * 128:h * 128 + 128])
                    _, rz = ring_load(wl[:, 3072 + h * 128:3072 + h * 128 + 128])

                    def qk(rw, rsw, bcol, bswcol, dst):
                        for (t0, n, isx) in blocks:
                            ps = proj(rw, t0, n, pbank())
                            if isx:
                                tx = t0 - LC
                                tb_ = rope_i[0]
                                rope_i[0] = 1 - tb_
                                abf = V(ABF[:, tb_ * 512:tb_ * 512 + n], [("ABF", tb_)])
                                t1 = V(TMP[2 * tb_][:, 0:n], ["TMP%d" % (2 * tb_)])
                                t2 = V(TMP[2 * tb_ + 1][:, 0:n], ["TMP%d" % (2 * tb_ + 1)])
                                p.act(abf, ps, AF.Identity, bias=vcol(l, bcol))
                                bk2 = pbank()
                                p.matmul(pv(bk2, 0, n), V(PERMB[:], ["PERMB"]), abf)
                                p.act(t2, pv(bk2, 0, n), AF.Copy)
                                p.tt(t1, abf, V(ROPE[:, tx:tx + n], ["ROPE"]), ALU.mult, eng="pool")
                                p.tt(t2, t2, V(ROPE[:, LX + tx:LX + tx + n], ["ROPE"]), ALU.mult, eng="pool")
                                p.tt(av(dst, t0, n), t1, t2, ALU.add, eng="pool")
                            else:
                                p.act(av(dst, t0, n), ps, AF.Identity, bias=vcol(l, bcol))
                            yield (4.2 if isx else 2.1) * n / 512

                    yield from qk(rw_k, None, V_BIN + 8 + h, None, KT)
                    for (t0, n, isx) in blocks:
                        nb = n // 128
                        bk_ = pbank()
                        transposes_bf16(lambda i: av(KT, t0 + i * 128, 128), nb, bk_)
                        p.act(av(KZF, t0, n), pv(bk_, 0, n, BF16), AF.Copy, scale=rcx(1, 0))
                        p.act(av(KZB, t0, n), pv(bk_, 0, n, BF16), AF.Copy, scale=rcx(1, 1))
                        yield 1.2 * n / 512
                    for (t0, n, isx) in blocks:
                        nb = n // 128
                        ps = proj(rv, t0, n, pbank())
                        vt = V(STB1[:, 0:n], ["STB1"])
                        p.act(vt, ps, AF.Identity, bias=vcol(l, V_BIN + 16 + h))
                        bk_ = pbank()
                        transposes_bf16(lambda i: V(STB1[:, i * 128:(i + 1) * 128], ["STB1"]), nb, bk_)
                        p.copy(av(VTK, t0, n), pv(bk_, 0, n, BF16), eng="act")
                        yield 2.8 * n / 512
                    yield from qk(rw_q, None, V_BIN + h, None, QT)
                    for (t0, n, isx) in blocks:
                        if not (last and not isx):
                            ps = proj(rz, t0, n, pbank())
                            p.act(av(SZ, t0, n), ps, AF.Silu, bias=vcol(l, V_BIN + 24 + h))
                            yield 2.1 * n / 512

                def chain_groups():
                    gf, gb = [], []
                    for (t0, n, isx) in blocks:
                        gf.append(list(range(t0 // 128, t0 // 128 + n // 128)))
                    cb = [g for g, b in zip(gf, blocks) if not b[2]]
                    xb = [g for g, b in zip(gf, blocks) if b[2]]
                    for g in reversed(cb):
                        gb.append(list(reversed(g)))
                    for g in reversed(xb):
                        gb.append(list(reversed(g)))
                    gf2 = [g for g, b in zip(gf, blocks) if not b[2]] + xb
                    return gf2, gb

                def C_ret(h, ws):
                    QT, KT, KZF, KZB, VTK, SZ, SBST, SFST = [ws + i for i in range(8)]
                    rcx = lambda kind, dd: V(RC[:, l * 48 + kind * 16 + dd * 8 + h:l * 48 + kind * 16 + dd * 8 + h + 1], [("RC", l)])
                    sfv = V(SF[:, 0:128], ["SF"])
                    sbv = V(SBK[:, 0:128], ["SBK"])
                    p.memset(sfv, 0.0)
                    p.memset(sbv, 0.0)
                    gf, gb = chain_groups()
                    kb = [3, 5]
                    for gi in range(len(gf)):
                        bf_, bb_ = kb[gi % 2], kb[gi % 2] + 1
                        for i, c in enumerate(gf[gi]):
                            p.matmul(pv(bf_, i * 128, 128), av(KZF, c * 128, 128), av(VTK, c * 128, 128))
                        for i, c in enumerate(gb[gi]):
                            p.matmul(pv(bb_, i * 128, 128), av(KZB, c * 128, 128), av(VTK, c * 128, 128))
                        for i in range(max(len(gf[gi]), len(gb[gi]))):
                            if i < len(gf[gi]):
                                p.copy(av(SFST, gf[gi][i] * 128, 128), sfv)
                            if i < len(gb[gi]):
                                p.copy(av(SBST, gb[gi][i] * 128, 128), sbv)
                            lastf = (gi == len(gf) - 1 and i == len(gf[gi]) - 1)
                            if i < len(gf[gi]) and not lastf:
                                p.stt(sfv, sfv, rcx(2, 0), pv(bf_, i * 128, 128), ALU.mult, ALU.add)
                            if i < len(gb[gi]) and not lastf:
                                p.stt(sbv, sbv, rcx(2, 1), pv(bb_, i * 128, 128), ALU.mult, ALU.add)
                        yield 1.3 * len(gf[gi])
                    for (t0, n, isx) in blocks:
                        nb = n // 128
                        c0 = t0 // 128
                        if last and not isx:
                            continue
                        for i in range(nb):
                            p.matmul(pv(3, i * 128, 128), av(KT, t0 + i * 128, 128), av(QT, t0 + i * 128, 128))
                        stv = av(ws + 10, 0, n)
                        p.tt(V(stv.ap.rearrange("p (a b) -> p a b", b=128), stv.keys), pv3(3, nb, 128),
                             bc_mid(V(DM[:], ["DM"]), [128, nb, 128]), ALU.mult)
                        for i in range(nb):
                            c = c0 + i
                            p.matmul(pv(5, i * 128, 128), av(QT, c * 128, 128), av(SFST, c * 128, 128))
                            p.matmul(pv(6, i * 128, 128), av(QT, c * 128, 128), av(SBST, c * 128, 128))
                        yield 2.0 * n / 512
                        for i in range(nb):
                            c = c0 + i
                            p.matmul(pv(4, i * 128, 128), V(stv.ap[:, i * 128:(i + 1) * 128], stv.keys), av(VTK, c * 128, 128))
                        rt = av(ws + 8, 0, n, F32)
                        p.ts(rt, pv(5, 0, n), rcx(0, 0), ALU.mult)
                        p.stt(rt, pv(6, 0, n), rcx(0, 1), rt, ALU.mult, ALU.add)
                        p.tt(rt, pv(4, 0, n), rt, ALU.add)
                        yield 2.5 * n / 512
                        head_norm_and_store(rt, nb, n, h, t0, vcol(l, V_RNW + h), SZ, ws, 3, ws + 9, 0, ws + 10, 512)
                        yield 8.0 * n / 512

                def P_ml(h, ws):
                    UQ, UK, MQT, MKT, MKTK, VPF, VPB, OT = [ws + i for i in range(8)]
                    SZm = UQ
                    _, rw_k = ring_load(wl[:, 5120 + h * 128:5120 + h * 128 + 128])
                    _, rw_q = ring_load(wl[:, 4096 + h * 128:4096 + h * 128 + 128])
                    _, rv = ring_load(wl[:, 6144 + h * 128:6144 + h * 128 + 128])
                    _, ro = ring_load(wl[:, 7168 + h * 128:7168 + h * 128 + 128])
                    _, rz = ring_load(wl[:, 8192 + h * 128:8192 + h * 128 + 128])
                    for qk_ in range(2):
                        for j in range(5):
                            p.act(V(DG[:, (qk_ * 5 + j) * 128:(qk_ * 5 + j + 1) * 128], [("DG", qk_)]), IDFv, AF.Copy,
                                  scale=vcol(l, V_CONVW + j * 16 + qk_ * 8 + h))
                    pcol = lambda t0, isx: t0 + (4 if isx else 2)
                    for ub in (UQ, UK):
                        p.memset(av(ub, 0, 2), 0.0, eng="pool")
                        p.memset(av(ub, LC + 2, 2), 0.0, eng="pool")
                        p.memset(av(ub, T + 4, 2), 0.0, eng="pool")
                    yield 1.0
                    for (rw, bcol, ub) in ((rw_k, V_BIN + 40 + h, UK), (rw_q, V_BIN + 32 + h, UQ)):
                        for (t0, n, isx) in blocks:
                            ps = proj(rw, t0, n, pbank())
                            p.act(av(ub, pcol(t0, isx), n), ps, AF.Identity, bias=vcol(l, bcol))
                            yield 2.1 * n / 512
                    for (qk_, ub, dst) in ((1, UK, MKT), (0, UQ, MQT)):
                        for (t0, n, isx) in blocks:
                            bk_ = pbank()
                            for j in range(5):
                                p.matmul(pv(bk_, 0, n), V(DG[:, (qk_ * 5 + j) * 128:(qk_ * 5 + j + 1) * 128], [("DG", qk_)]),
                                         av(ub, pcol(t0, isx) - 2 + j, n), start=(j == 0), stop=(j == 4))
                            p.act(av(dst, t0, n), pv(bk_, 0, n), AF.Silu, bias=vcol(l, V_CONVB + qk_ * 8 + h))
                            yield 1.5 * n / 512
                    for (t0, n, isx) in blocks:
                        nb = n // 128
                        bk_ = pbank()
                        transposes_bf16(lambda i: av(MKT, t0 + i * 128, 128), nb, bk_)
                        p.copy(av(MKTK, t0, n), pv(bk_, 0, n, BF16), eng="act")
                        yield 1.0 * n / 512
                    ew3 = EW[:].rearrange("p (n r) -> p n r", r=16)
                    for (t0, n, isx) in blocks:
                        nb = n // 128
                        c0 = t0 // 128
                        ps = proj(rv, t0, n, pbank())
                        vt = V(STB1[:, 0:n], ["STB1"])
                        p.act(vt, ps, AF.Identity, bias=vcol(l, V_BIN + 48 + h))
                        bk_ = pbank()
                        transposes_bf16(lambda i: V(STB1[:, i * 128:(i + 1) * 128], ["STB1"]), nb, bk_)
                        for dd, slot in ((0, VPF), (1, VPB)):
                            ewc = V(ew3[:, c0:c0 + nb, dd * 8 + h:dd * 8 + h + 1], ["EW"])
                            for i in range(nb):
                                p.act(av(slot, (c0 + i) * 129, 128), pv(bk_, i * 128, 128, BF16), AF.Copy,
                                      scale=V(ew3[:, c0 + i, dd * 8 + h:dd * 8 + h + 1], ["EW"]))
                            p.copy(av3(slot, c0 * 129, nb, 129, stride=129, sub=(128, 129)), ewc, eng="pool")
                        yield 4.0 * n / 512
                    for (t0, n, isx) in blocks:
                        if not (last and not isx):
                            ps = proj(ro, t0, n, pbank())
                            p.act(av(OT, t0, n), ps, AF.Sigmoid, bias=vcol(l, V_BIN + 56 + h))
                            yield 2.1 * n / 512
                    for (t0, n, isx) in blocks:
                        if not (last and not isx):
                            ps = proj(rz, t0, n, pbank())
                            p.act(av(SZm, t0, n), ps, AF.Silu, bias=vcol(l, V_BIN + 64 + h))
                            yield 2.1 * n / 512

                def C_ml(h, ws):
                    UQ, UK, MQT, MKT, MKTK, VPF, VPB, OT = [ws + i for i in range(8)]
                    SZm, CBST, CFST = UQ, UK, ws + 10
                    al3 = ALB[:].rearrange("p (n r) -> p n r", r=16)
                    alc = lambda n_, dd: V(al3[:, n_, dd * 8 + h:dd * 8 + h + 1], ["ALB"])
                    cfv = V(SF[:, 0:129], ["SF"])
                    cbv = V(SBK[:, 0:129], ["SBK"])
                    p.memset(cfv, 0.0)
                    p.memset(cbv, 0.0)
                    gf, gb = chain_groups()
                    for gi in range(len(gf)):
                        for i, c in enumerate(gf[gi]):
                            p.matmul(pv(3 + i // 2, (i % 2) * 256, 129), av(MKTK, c * 128, 128), av(VPF, c * 129, 129))
                        for i, c in enumerate(gb[gi]):
                            p.matmul(pv(5 + i // 2, (i % 2) * 256, 129), av(MKTK, c * 128, 128), av(VPB, c * 129, 129))
                        for i in range(max(len(gf[gi]), len(gb[gi]))):
                            lastf = (gi == len(gf) - 1 and i == len(gf[gi]) - 1)
                            if i < len(gf[gi]):
                                c = gf[gi][i]
                                p.ts(av(CFST, c * 129, 129), cfv, alc(c, 0), ALU.mult)
                            if i < len(gb[gi]):
                                c = gb[gi][i]
                                p.ts(av(CBST, c * 129, 129), cbv, alc(c, 1), ALU.mult)
                            if i < len(gf[gi]) and not lastf:
                                c = gf[gi][i]
                                p.stt(cfv, cfv, alc(c, 0), pv(3 + i // 2, (i % 2) * 256, 129), ALU.mult, ALU.add)
                            if i < len(gb[gi]) and not lastf:
                                c = gb[gi][i]
                                p.stt(cbv, cbv, alc(c, 1), pv(5 + i // 2, (i % 2) * 256, 129), ALU.mult, ALU.add)
                        yield 1.4 * len(gf[gi])
                    th3 = THR[:].rearrange("p (n r) -> p n r", r=16)
                    for (t0, n, isx) in blocks:
                        nb = n // 128
                        c0 = t0 // 128
                        if last and not isx:
                            continue
                        for i in range(nb):
                            p.matmul(pv(3, i * 128, 128), av(MKT, t0 + i * 128, 128), av(MQT, t0 + i * 128, 128))
                        stb_ = av(ws + 9, 0, n)
                        stf = av(ws + 9, 512, n)
                        p.tt(V(stf.ap.rearrange("p (a b) -> p a b", b=128), stf.keys), pv3(3, nb, 128),
                             bc_mid(cstv(C_MF), [128, nb, 128]), ALU.mult)
                        p.tt(V(stb_.ap.rearrange("p (a b) -> p a b", b=128), stb_.keys), pv3(3, nb, 128),
                             bc_mid(cstv(C_MB), [128, nb, 128]), ALU.mult)
                        yield 2.5 * n / 512
                        ot = av(ws + 8, 0, n, F32)
                        ot3 = V(ot.ap.rearrange("p (a b) -> p a b", b=128), ot.keys)
                        t1 = av(ws + 8, 1024, n, F32)
                        t13 = V(t1.ap.rearrange("p (a b) -> p a b", b=128), t1.keys)

                        def one_dir(dd, stv_, vp_, cst_, dstv):
                            for i in range(nb):
                                c = c0 + i
                                b_, c_ = 4 + i // 2, (i % 2) * 256
                                p.matmul(pv(b_, c_, 129), V(stv_.ap[:, i * 128:(i + 1) * 128], stv_.keys), av(vp_, c * 129, 129), start=True, stop=False)
                                p.matmul(pv(b_, c_, 129), av(MQT, c * 128, 128), av(cst_, c * 129, 129), start=False, stop=True)
                            dn = smc(24 + dd * 4, nb)
                            ab = smc(32 + dd * 4, nb)
                            for bq in range((nb + 1) // 2):
                                nn = min(2, nb - 2 * bq)
                                num = PS[4 + bq][:, 0:nn * 256].rearrange("p (a b) -> p a b", b=256)
                                p.copy(V(SM[:, 24 + dd * 4 + 2 * bq:24 + dd * 4 + 2 * bq + nn], [("SM", 24 + dd * 4)]),
                                       V(num[:, :, 128], [("P", 4 + bq)]))
                            p.stt(ab, dn, -1.0, dn, ALU.mult, ALU.max)
                            p.tt(ab, ab, V(th3[:, c0:c0 + nb, dd * 8 + h], ["THR"]), ALU.max)
                            p.recip(ab, ab)
                            if dd == 0:
                                for bq in range((nb + 1) // 2):
                                    nn = min(2, nb - 2 * bq)
                                    num = PS[4 + bq][:, 0:nn * 256].rearrange("p (a b) -> p a b", b=256)
                                    p.tt(V(dstv.ap[:, 2 * bq:2 * bq + nn, :], dstv.keys), V(num[:, :, 0:128], [("P", 4 + bq)]),
                                         V(SM[:, 32 + dd * 4 + 2 * bq:32 + dd * 4 + 2 * bq + nn].unsqueeze(2).to_broadcast([128, nn, 128]),
                                           [("SM", 32 + dd * 4)]), ALU.mult)
                            else:
                                for i in range(nb):
                                    p.stt(V(dstv.ap[:, i, :], dstv.keys), pv(4 + i // 2, (i % 2) * 256, 128),
                                          V(SM[:, 32 + dd * 4 + i:33 + dd * 4 + i], [("SM", 32 + dd * 4)]),
                                          V(dstv.ap[:, i, :], dstv.keys), ALU.mult, ALU.add)

                        one_dir(0, stf, VPF, CFST, ot3)
                        yield 3.5 * n / 512
                        one_dir(1, stb_, VPB, CBST, ot3)
                        yield 3.5 * n / 512
                        transposes_bf16(lambda i: av(OT, t0 + i * 128, 128), nb, 6)
                        p.tt(ot, ot, pv(6, 0, n, BF16), ALU.mult)
                        yield 2.5 * n / 512
                        head_norm_and_store(ot, nb, n, 8 + h, t0, vcol(l, V_MNW + h), SZm, ws, 3, ws + 8, 1024, ws + 9, 1024)
                        yield 8.0 * n / 512

                units = []
                for h in range(8):
                    units.append((P_ret, C_ret, h))
                    units.append((P_ml, C_ml, h))
                wsof = lambda u: WS_BASE + WS_N * (u % 2)
                MW0 = 8 * SW

                def mw(i):
                    a = MW0 + i * 8192
                    return ARENA[:, a:a + 8192].rearrange("p (k n) -> p k n", n=1024), fkeys(a, a + 8192)

                MW = []

                def merge_weight_loads(i0, i1):
                    wsrcs = [w_ro[l], w_mo[l], wl[:, 9248:10272], wl[:, 10272:11296], w_out[l]]
                    for i in range(i0, i1):
                        wsrc = wsrcs[i]
                        ap_, ks_ = mw(i)
                        p.dma(V(ap_, ks_), V(wsrc.rearrange("(k p) n -> p k n", p=128), ["wdram"]), eng="pool")
                        MW.append((ap_, ks_))
                        yield

                Pstream = p.collect(units[0][0](units[0][2], wsof(0)))
                Cstream = p.collect(gate_prep())
                for u in range(len(units)):
                    if u + 1 < len(units):
                        Pstream += p.collect(units[u + 1][0](units[u + 1][2], wsof(u + 1)))
                    Cstream += p.collect(units[u][1](units[u][2], wsof(u)))
                Pstream += p.collect(merge_weight_loads(0, 5))
                p.schedule([Pstream, Cstream])

                if dbg and s == NSEQ - 1 and last:
                    pass

                YS0 = 26 * SW
                YT0 = 30 * SW
                GBX = V(RING[:, 0:2048].bitcast(F32), [("R", 0), ("R", 1)])
                GBC = V(RING[:, 2048:4096].bitcast(F32), [("R", 2), ("R", 3)])
                XS_ = V(RING[:, 4096:6144].bitcast(F32), [("R", 4), ("R", 5)])
                XO_ = V(XO[:], ["XO"])
                if last:
                    p.dma(GBC, V(fnw_d, ["d_fnw"]))
                for (gb, who) in ((GBX, s), (GBC, NSEQ)):
                    if last and who == NSEQ:
                        continue
                    for j in range(8):
                        c_ = (l * 24 + 16 + j) * NS1 + who
                        gt_ = V(TMP[0][:, 0:128], ["TMP0"])
                        p.ts(gt_, IDFv, V(MODF[:, c_:c_ + 1], [("MODF", l)]), ALU.mult)
                        p.matmul(pv(j // 4, (j % 4) * 128, 128), V(ONES[:], ["ONES"]), gt_)
                    for half in range(2):
                        p.copy(V(gb.ap[:, half * 512:(half + 1) * 512], gb.keys), pv(half, 0, 512), eng="act")
                mblocks = [b for b in blocks if not (last and not b[2])]
                hbs = []
                for (t0, n, isx) in mblocks:
                    for o in range(0, n, 256):
                        hbs.append((t0 + o, min(256, n - o), isx))

                def ysb_view(bi, n):
                    a0 = YS0 + bi * 4096
                    return ARENA[:, a0:a0 + 4096].rearrange("p (k n) -> p k n", n=256)[:, :, 0:n], fkeys(a0, a0 + 4096)

                def ytb_view(bi):
                    a0 = YT0 + bi * 2048
                    return ARENA[:, a0:a0 + 2048].rearrange("p (k n) -> p k n", n=256), fkeys(a0, a0 + 2048)

                def J_stage(hb):
                    t0, n, isx = hbs[hb]
                    bi = hb % 2
                    ysb_ap, ysb_k = ysb_view(bi, n)
                    ytb_ap, ytb_k = ytb_view(bi)
                    p.dma(V(ysb_ap, ysb_k), V(ys[:, :, t0:t0 + n].rearrange("k p t -> p k t"), [("ys", k_, (t0 // 512) * 512) for k_ in range(16)]))
                    for j in range(8):
                        c0 = (j % 2) * 256
                        for k in range(8):
                            p.matmul(pv(0, c0, n), V(MW[0][0][:, k, j * 128:(j + 1) * 128], MW[0][1]), V(ysb_ap[:, k, :], ysb_k), start=(k == 0), stop=(k == 7))
                        for k in range(8):
                            p.matmul(pv(1, c0, n), V(MW[1][0][:, k, j * 128:(j + 1) * 128], MW[1][1]), V(ysb_ap[:, 8 + k, :], ysb_k), start=(k == 0), stop=(k == 7))
                        for k in range(8):
                            p.matmul(pv(2, c0, n), V(MW[2][0][:, k, j * 128:(j + 1) * 128], MW[2][1]), hT(k, t0, n), start=(k == 0), stop=(k == 7))
                        for k in range(8):
                            p.matmul(pv(3, c0, n), V(MW[3][0][:, k, j * 128:(j + 1) * 128], MW[3][1]), hT(k, t0, n), start=(k == 0), stop=(k == 7))
                        sg1 = V(TMP[0][:, c0:c0 + n], [("TMP0", j % 2)])
                        sg2 = V(TMP[1][:, c0:c0 + n], [("TMP1", j % 2)])
                        p.act(sg1, pv(2, c0, n), AF.Sigmoid, bias=vcol(l, V_BG + j))
                        p.act(sg2, pv(3, c0, n), AF.Sigmoid, bias=vcol(l, V_BG + 8 + j))
                        p.tt(sg1, pv(0, c0, n), sg1, ALU.mult)
                        p.tt(sg2, pv(1, c0, n), sg2, ALU.mult)
                        p.tt(V(ytb_ap[:, j, 0:n], ytb_k), sg1, sg2, ALU.add)
                        yield

                otile = [0]

                def O_stage(hb):
                    t0, n, isx = hbs[hb]
                    bi = hb % 2
                    ytb_ap, ytb_k = ytb_view(bi)
                    for ti in range(n // 128):
                        i = t0 // 128 + ti
                        gb = GBX if isx else GBC
                        par = otile[0] % 2
                        otile[0] += 1
                        if par == 0:
                            xs_h = [V(XS_.ap[:, hf * 512:(hf + 1) * 512], [("R", 4 + hf)]) for hf in range(2)]
                        else:
                            xs_h = [V(TMP[2 + hf][:, :], ["TMP%d" % (2 + hf)]) for hf in range(2)]
                        for hf in range(2):
                            p.dma(xs_h[hf], V(src[s, i * 128:(i + 1) * 128, hf * 512:(hf + 1) * 512], [("xs", s, i)]))
                        for half in range(2):
                            bk_ = 4 + par * 2 + half
                            for k in range(8):
                                p.matmul(pv(bk_, 0, 512), V(ytb_ap[:, k, ti * 128:(ti + 1) * 128], ytb_k),
                                         V(MW[4][0][:, k, half * 512:(half + 1) * 512], MW[4][1]), start=(k == 0), stop=(k == 7))
                            xo_h = V(XO_.ap[:, half * 512:(half + 1) * 512], XO_.keys)
                            p.tt(xo_h, pv(bk_, 0, 512), V(gb.ap[:, half * 512:(half + 1) * 512], gb.keys), ALU.mult)
                            p.tt(xo_h, xo_h, xs_h[half], ALU.add)
                        if not last:
                            p.dma(V(xs[s, i * 128:(i + 1) * 128, :], [("xs", s, i)]), XO_)
                        else:
                            junk_ = V(STB1[:, :].bitcast(F32), ["STB1"]) if False else xs_h[0]
                            p.memset(smc(3), 0.0)
                            p.memset(smc(4), 0.0)
                            for hf in range(2):
                                p.act(xs_h[hf], V(XO_.ap[:, hf * 512:(hf + 1) * 512], XO_.keys), AF.Square, accum_out=smc(3 + hf))
                            p.tt(smc(0), smc(3), smc(4), ALU.add)
                            p.ts(smc(0), smc(0), 1.0 / D, ALU.mult, EPS, ALU.add)
                            rsqrt_dve(smc(1), smc(0), smc(2))
                            p.act(XO_, XO_, AF.Copy, scale=smc(1))
                            p.tt(XO_, XO_, GBC, ALU.mult)
                            p.dma(V(out[s, (i - NCC) * 128:(i - NCC + 1) * 128, :], [("out", s, i)]), XO_)
                        yield

                Js, Os = [], []
                nh = len(hbs)
                order = []
                for hb in range(nh):
                    order.append(("J", hb))
                    if hb >= 1:
                        order.append(("O", hb - 1))
                order.append(("O", nh - 1))
                for kind, hb in order:
                    if kind == "J":
                        Js += p.collect(J_stage(hb))
                    else:
                        Os += p.collect(O_stage(hb))
                p.schedule([Js, Os])

        p.emit()
    return nc, len(p.ops)


def _host_consts(LX):
    k = np.arange(128, dtype=np.float32)[:, None]
    q = np.arange(128, dtype=np.float32)[None, :]
    cst = np.zeros((128, NCST), np.float32)
    cst[:, C_ID:C_ID + 128] = np.eye(128, dtype=np.float32)
    cst[:, C_MF:C_MF + 128] = (k <= q)
    cst[:, C_MB:C_MB + 128] = (k >= q)
    cst[:, C_QMK:C_QMK + 128] = np.maximum(q - k, 0)
    cst[:, C_KMQ:C_KMQ + 128] = np.maximum(k - q, 0)
    pidx = np.arange(128, dtype=np.float32)
    cst[:, C_POS + 0] = pidx + 1
    cst[:, C_POS + 1] = 128 - pidx
    cst[:, C_POS + 2] = 127 - pidx
    cst[:, C_POS + 3] = pidx
    cst[:, C_POS + 4] = 128.0
    GRID_W = 64
    rows_n = LX // GRID_W
    rows = np.repeat(np.arange(rows_n, dtype=np.float32), GRID_W)
    cols = np.tile(np.arange(GRID_W, dtype=np.float32), rows_n)
    nf = 32
    freqs = (np.float32(10000.0) ** (-np.arange(nf, dtype=np.float32) / np.float32(nf))).astype(np.float32)
    ang = np.concatenate([rows[:, None] * freqs, cols[:, None] * freqs], axis=-1).astype(np.float32)
    cos = np.cos(ang).astype(np.float32).T
    sin = np.sin(ang).astype(np.float32).T
    rope = np.zeros((2, 128, LX), np.float32)
    rope[0, :64] = cos
    rope[0, 64:] = cos
    rope[1, :64] = -sin
    rope[1, 64:] = sin
    return cst, rope


def _host_tables(norm_w, b_ada, b_in, conv_w, conv_b, ret_norm_w, ml_norm_w, ret_log_gamma):
    depth = norm_w.shape[0]
    vtt = np.zeros((depth, 128, NV), np.float32)
    bgr = np.zeros((depth, 128, 32), np.float32)
    for l in range(depth):
        vtt[l, :, V_BADA:V_BADA + 24] = b_ada[l].reshape(24, 128).T
        vtt[l, :, V_NORMW:V_NORMW + 8] = norm_w[l].reshape(8, 128).T
        vtt[l, :, V_BIN:V_BIN + 72] = b_in[l][:GATE_OFF].reshape(72, 128).T
        vtt[l, :, V_BG:V_BG + 16] = b_in[l][GATE_OFF + 32:].reshape(16, 128).T
        vtt[l, :, V_CONVB:V_CONVB + 16] = conv_b[l].reshape(16, 128).T
        vtt[l, :, V_CONVW:V_CONVW + 80] = conv_w[l].reshape(80, 128).T
        vtt[l, :, V_RNW:V_RNW + 8] = ret_norm_w[l].reshape(8, 128).T
        vtt[l, :, V_MNW:V_MNW + 8] = ml_norm_w[l].reshape(8, 128).T
        bq = b_in[l][:2048].reshape(16, 128)
        bsw = np.concatenate([bq[:, 64:], bq[:, :64]], axis=1)
        vtt[l, :, V_BSW:V_BSW + 16] = bsw.T
        bgr[l] = np.tile(b_in[l][GATE_OFF:GATE_OFF + 32][None, :], (128, 1))
    lgb = np.tile(ret_log_gamma.reshape(1, depth * 16), (128, 1)).astype(np.float32)
    return vtt, bgr, lgb


_CACHE = {}


def run_config(inputs, n_cores, NSEQ, LC, LX, DEPTH):
    f = lambda a: np.ascontiguousarray(np.asarray(a, dtype=np.float32))
    x, c, ctx, c_ctx = f(inputs["x"]), f(inputs["c"]), f(inputs["ctx"]), f(inputs["c_ctx"])
    key = (NSEQ, LC, LX, DEPTH)
    if key not in _CACHE:
        _CACHE[key] = build_program(NSEQ, LC, LX, DEPTH)
    nc, nops = _CACHE[key]
    cst, rope = _host_consts(LX)
    vtt, bgr, lgb = _host_tables(f(inputs["norm_w"]), f(inputs["b_ada"]), f(inputs["b_in"]), f(inputs["conv_w"]),
                                 f(inputs["conv_b"]), f(inputs["ret_norm_w"]), f(inputs["ml_norm_w"]),
                                 f(inputs["ret_log_gamma"]))
    fnw = np.ascontiguousarray(np.tile(f(inputs["final_norm_w"])[None, :], (128, 1)))
    shared = {"w_ada": f(inputs["w_ada"]), "w_in": f(inputs["w_in"]), "w_ret_o": f(inputs["w_ret_o"]),
              "w_ml_o": f(inputs["w_ml_o"]), "w_out": f(inputs["w_out"]), "vtt": vtt, "bgr": bgr, "lgb": lgb,
              "cst": cst, "rope": rope, "fnw": fnw}
    in_maps = []
    for ci in range(n_cores):
        sl = slice(ci * NSEQ, (ci + 1) * NSEQ)
        xin = np.ascontiguousarray(np.concatenate([ctx[sl], x[sl]], axis=1))
        cc = np.concatenate([c[sl], c_ctx[None, :]], axis=0)
        cT = np.ascontiguousarray(cc.reshape(NSEQ + 1, 8, 128).transpose(2, 1, 0).reshape(128, 8 * (NSEQ + 1)))
        m = dict(shared)
        m["xin"] = xin
        m["cT"] = cT
        in_maps.append(m)
    res = run_bass_kernel_spmd(nc, in_maps, core_ids=list(range(n_cores)))
    return np.concatenate([r["out"] for r in res.results], axis=0)


def kernel(**inputs):
    return run_config(inputs, 8, 4, 256, 2048, 2)
```

```python
import contextlib
import math
import numpy as np
import concourse.bass as bass
import concourse.mybir as mybir
from concourse.bass_utils import run_bass_kernel_spmd

F32 = mybir.dt.float32
BF16 = mybir.dt.bfloat16
I32 = mybir.dt.int32
AF = mybir.ActivationFunctionType
ALU = mybir.AluOpType
AX = mybir.AxisListType

N_DMA_SEMS = 24
D = 1024
DIN = 11296
GATE_OFF = 9216
EPS = 1e-6
SW = 2336
NRING = 6
CSCALE = 128.0 ** -0.5


class V:
    __slots__ = ("ap", "keys")

    def __init__(self, ap, keys):
        self.ap = ap
        self.keys = tuple(keys)


class Op:
    __slots__ = ("eng", "fn", "deps", "is_dma", "tick", "needed", "dsem", "dval", "idx")

    def __init__(self, eng, fn, is_dma, idx):
        self.eng = eng
        self.fn = fn
        self.deps = None
        self.is_dma = is_dma
        self.tick = None
        self.needed = False
        self.dsem = None
        self.dval = None
        self.idx = idx


class Prog:
    ENGS = ("pe", "act", "dve", "pool", "sp")

    def __init__(self, nc):
        self.nc = nc
        self.ops = []
        self.last_w = {}
        self.readers = {}
        self.ndma = 0
        self.dma_ops = []
        self.defer = None
        self.refseq = 0

    def collect(self, gen):
        assert self.defer is None
        self.defer = []
        for _ in gen:
            pass
        lst = self.defer
        self.defer = None
        return lst

    def schedule(self, streams, lat=0.3):
        from collections import deque
        ns = len(streams)
        pend_w = [dict() for _ in range(ns)]
        pend_a = [dict() for _ in range(ns)]
        for si, ops in enumerate(streams):
            for op in ops:
                rs = op[6]
                for v in op[2]:
                    for k in v.keys:
                        pend_a[si].setdefault(k, deque()).append(rs)
                for v in op[3]:
                    for k in v.keys:
                        pend_a[si].setdefault(k, deque()).append(rs)
                        pend_w[si].setdefault(k, deque()).append(rs)
        eng_free = {}
        wend, rend = {}, {}
        idx = [0] * ns
        total = sum(len(x) for x in streams)
        for _ in range(total):
            best = None
            for si, ops in enumerate(streams):
                i = idx[si]
                if i >= len(ops):
                    continue
                eng, fn, reads, writes, is_dma, cost, rs = ops[i]
                blocked = False
                if ns > 1:
                    for ti in range(ns):
                        if ti == si:
                            continue
                        pw, pa = pend_w[ti], pend_a[ti]
                        for v in reads:
                            for k in v.keys:
                                d = pw.get(k)
                                if d and d[0] < rs:
                                    blocked = True
                                    break
                            if blocked:
                                break
                        if blocked:
                            break
                        for v in writes:
                            for k in v.keys:
                                d = pa.get(k)
                                if d and d[0] < rs:
                                    blocked = True
                                    break
                            if blocked:
                                break
                        if blocked:
                            break
                if blocked:
                    continue
                ready = 0.0
                for v in reads:
                    for k in v.keys:
                        t = wend.get(k)
                        if t is not None and t > ready:
                            ready = t
                for v in writes:
                    for k in v.keys:
                        t = wend.get(k)
                        if t is not None and t > ready:
                            ready = t
                        t = rend.get(k)
                        if t is not None and t > ready:
                            ready = t
                start = max(ready + lat, eng_free.get(eng, 0.0))
                if best is None or start < best[0]:
                    best = (start, si)
            assert best is not None, "scheduler deadlock"
            start, si = best
            op = streams[si][idx[si]]
            idx[si] += 1
            eng, fn, reads, writes, is_dma, cost, rs = op
            for v in reads:
                for k in v.keys:
                    pend_a[si][k].popleft()
            for v in writes:
                for k in v.keys:
                    pend_a[si][k].popleft()
                    pend_w[si][k].popleft()
            if is_dma:
                eng_free[eng] = start + (0.6 if eng == "pool" else 0.15)
                end = start + cost
            else:
                end = start + cost
                eng_free[eng] = end
            for v in reads:
                for k in v.keys:
                    if rend.get(k, 0.0) < end:
                        rend[k] = end
            for v in writes:
                for k in v.keys:
                    wend[k] = end
            self.add(eng, fn, reads, writes, is_dma, cost)

    @staticmethod
    def _n(ap):
        n = 1
        for d in ap.shape[1:]:
            n *= d
        return n

    def add(self, eng, fn, reads, writes, is_dma=False, cost=0.3):
        if self.defer is not None:
            self.refseq += 1
            self.defer.append((eng, fn, reads, writes, is_dma, cost, self.refseq))
            return None
        idx = len(self.ops)
        deps = set()
        for v in reads:
            for k in v.keys:
                w = self.last_w.get(k)
                if w is not None:
                    deps.add(w)
        for v in writes:
            for k in v.keys:
                w = self.last_w.get(k)
                if w is not None:
                    deps.add(w)
                r = self.readers.get(k)
                if r:
                    deps.update(r.values())
        op = Op(eng, fn, is_dma, idx)
        if is_dma:
            j = self.ndma
            self.ndma += 1
            op.dsem = j % N_DMA_SEMS
            op.dval = 16 * (j // N_DMA_SEMS + 1)
            if j >= N_DMA_SEMS:
                deps.add(self.dma_ops[j - N_DMA_SEMS])
            self.dma_ops.append(idx)
        if eng == "pe" and not is_dma:
            deps = {d for d in deps if not (self.ops[d].eng == "pe" and not self.ops[d].is_dma)}
        op.deps = deps
        for d in deps:
            self.ops[d].needed = True
        for v in reads:
            for k in v.keys:
                rk = ("dma", idx) if is_dma else eng
                self.readers.setdefault(k, {})[rk] = idx
        for v in writes:
            for k in v.keys:
                self.last_w[k] = idx
                self.readers[k] = {}
        self.ops.append(op)
        return op

    def emit(self, final_dma_ops=()):
        nc = self.nc
        ops = self.ops
        cnt = {e: 0 for e in self.ENGS}
        for op in ops:
            if op.needed and not op.is_dma:
                cnt[op.eng] += 1
                op.tick = cnt[op.eng]
        by_eng = {e: [] for e in self.ENGS}
        for op in ops:
            by_eng[op.eng].append(op)
        with contextlib.ExitStack() as st:
            esem = {e: st.enter_context(nc.semaphore("s_" + e)) for e in self.ENGS}
            dsem = [st.enter_context(nc.semaphore("d%d" % i)) for i in range(N_DMA_SEMS)]
            block = st.enter_context(nc.Block())

            def run(engname, eng):
                waited = {}
                for op in by_eng[engname]:
                    for d in sorted(op.deps):
                        dop = ops[d]
                        if dop.is_dma:
                            key = ("d", dop.dsem)
                            sem, val = dsem[dop.dsem], dop.dval
                        else:
                            key = ("e", dop.eng)
                            sem, val = esem[dop.eng], dop.tick
                        if waited.get(key, 0) >= val:
                            continue
                        eng.wait_ge(sem, val)
                        waited[key] = val
                    ins = op.fn(eng)
                    if op.is_dma:
                        ins.then_inc(dsem[op.dsem], 16)
                    elif op.needed:
                        ins.then_inc(esem[op.eng], 1)
                if engname == "sp":
                    fin = {}
                    for o_ in ops:
                        if o_.is_dma:
                            fin[o_.dsem] = max(fin.get(o_.dsem, 0), o_.dval)
                    for ds_, dv_ in sorted(fin.items()):
                        eng.wait_ge(dsem[ds_], dv_)

            @block.tensor
            def _(e):
                run("pe", e)

            @block.scalar
            def _(e):
                run("act", e)

            @block.vector
            def _(e):
                run("dve", e)

            @block.gpsimd
            def _(e):
                run("pool", e)

            @block.sync
            def _(e):
                run("sp", e)

    def dma(self, out, in_, eng="sp"):
        nbytes = self._n(out.ap) * 128 * 4
        return self.add(eng, lambda e: e.dma_start(out=out.ap, in_=in_.ap), [in_], [out], is_dma=True,
                        cost=3.0 + nbytes / 150e3)

    def matmul(self, out, lhsT, rhs, start=True, stop=True):
        return self.add("pe", lambda e: e.matmul(out.ap, lhsT.ap, rhs.ap, start=start, stop=stop),
                        [lhsT, rhs], [out], cost=max(0.11, self._n(rhs.ap) * 0.00047 + 0.02))

    def transpose(self, out, in_, ident):
        return self.add("pe", lambda e: e.transpose(out.ap, in_.ap, ident.ap), [in_, ident], [out], cost=0.11)

    def act(self, out, in_, func, bias=None, scale=None, accum_out=None):
        reads = [in_]
        kw = {}
        if isinstance(bias, V):
            reads.append(bias)
            kw["bias"] = bias.ap
        elif bias is not None:
            kw["bias"] = bias
        if isinstance(scale, V):
            reads.append(scale)
            kw["scale"] = scale.ap
        elif scale is not None:
            kw["scale"] = scale
        writes = [out]
        if accum_out is not None:
            writes.append(accum_out)
            kw["accum_out"] = accum_out.ap
        return self.add("act", lambda e: e.activation(out.ap, in_.ap, func, **kw), reads, writes,
                        cost=(self._n(out.ap) + 335) / 1200.0 + (1.3 if func == AF.Sqrt else 0.0))

    def tt(self, out, in0, in1, op, eng="dve"):
        return self.add(eng, lambda e: e.tensor_tensor(out.ap, in0.ap, in1.ap, op), [in0, in1], [out], cost=self._ec(eng, out))

    def ts(self, out, in0, s1, op0, s2=None, op1=None, eng="dve"):
        reads = [in0]
        a1 = s1
        if isinstance(s1, V):
            reads.append(s1)
            a1 = s1.ap
        a2 = s2
        if isinstance(s2, V):
            reads.append(s2)
            a2 = s2.ap
        if op1 is None:
            return self.add(eng, lambda e: e.tensor_single_scalar(out.ap, in0.ap, a1, op0), reads, [out], cost=self._ec(eng, out))
        return self.add(eng, lambda e: e.tensor_scalar(out.ap, in0.ap, a1, a2, op0, op1), reads, [out], cost=self._ec(eng, out))

    def stt(self, out, in0, scalar, in1, op0, op1, eng="dve"):
        reads = [in0, in1]
        a = scalar
        if isinstance(scalar, V):
            reads.append(scalar)
            a = scalar.ap
        return self.add(eng, lambda e: e.scalar_tensor_tensor(out.ap, in0.ap, a, in1.ap, op0, op1), reads, [out], cost=self._ec(eng, out))

    def copy(self, out, in_, eng="dve"):
        if eng == "act":
            return self.add("act", lambda e: e.copy(out.ap, in_.ap), [in_], [out], cost=(self._n(out.ap) + 335) / 1200.0)
        return self.add(eng, lambda e: e.tensor_copy(out.ap, in_.ap), [in_], [out], cost=self._ec(eng, out))

    def reduce(self, out, in_, op, eng="dve"):
        return self.add(eng, lambda e: e.tensor_reduce(out.ap, in_.ap, AX.X, op), [in_], [out], cost=(self._n(in_.ap) + 150) / 960.0)

    def memset(self, out, val, eng="dve"):
        return self.add(eng, lambda e: e.memset(out.ap, val), [], [out], cost=self._ec(eng, out))

    def recip(self, out, in_):
        return self.add("dve", lambda e: e.reciprocal(out.ap, in_.ap), [in_], [out], cost=self._ec("dve", out))

    def _ec(self, eng, out):
        n = self._n(out.ap)
        if eng == "pool":
            return (2.4 * n + 150) / 960.0
        return 1.9 * (n + 150) / 960.0


V_BADA = 0
V_NORMW = 24
V_BIN = 32
V_BG = 104
V_CONVB = 120
V_CONVW = 136
V_RNW = 216
V_MNW = 224
V_BSW = 232
NV = 248
C_ID, C_MF, C_MB, C_QMK, C_KMQ, C_POS = 0, 128, 256, 384, 512, 640
NCST = 648


def build_program(NSEQ, LC, LX, DEPTH, dbg=False):
    nc = bass.Bass("TRN2", target_bir_lowering=False)
    T = LC + LX
    NCH = T // 128
    NCC = LC // 128
    NS1 = NSEQ + 1
    LNC = math.log(CSCALE)

    def dram(name, shape, kind="ExternalInput", dt=F32):
        return nc.dram_tensor(name, list(shape), dt, kind=kind).ap()

    xin = dram("xin", [NSEQ, T, D])
    out = dram("out", [NSEQ, LX, D], kind="ExternalOutput")
    xs = dram("xs", [NSEQ, T, D], kind="Internal")
    ys = dram("ys", [16, 128, T], kind="Internal", dt=BF16)
    cT = dram("cT", [128, 8 * NS1])
    w_ada = dram("w_ada", [DEPTH, D, 3 * D])
    w_in = dram("w_in", [DEPTH, D, DIN])
    w_ro = dram("w_ret_o", [DEPTH, D, D])
    w_mo = dram("w_ml_o", [DEPTH, D, D])
    w_out = dram("w_out", [DEPTH, D, D])
    vtt = dram("vtt", [DEPTH, 128, NV])
    bgr = dram("bgr", [DEPTH, 128, 32])
    lgb_d = dram("lgb", [128, DEPTH * 16])
    cst_d = dram("cst", [128, NCST])
    rope_d = dram("rope", [2, 128, LX])
    fnw_d = dram("fnw", [128, D])
    dbg_outs = {}

    blocks = []
    for i in range(0, LC, 512):
        blocks.append((i, min(512, LC - i), False))
    for i in range(0, LX, 512):
        blocks.append((LC + i, min(512, LX - i), True))
    fwd_order = list(range(NCH))
    bwd_order = list(reversed(range(NCC))) + list(reversed(range(NCC, NCH)))

    st = contextlib.ExitStack()
    with st:
        def sb(name, shape, dt=F32):
            return st.enter_context(nc.sbuf_tensor(name, list(shape), dt))

        p = Prog(nc)
        NSLOT = 32
        ARENA = sb("arena", [128, NSLOT * SW], BF16)
        RING = sb("ring", [128, NRING * 1024], BF16)
        ROPE = sb("rope_sb", [128, 2 * LX], F32)
        CST = sb("cst_sb", [128, NCST], F32)
        IDB = sb("idb", [128, 128], BF16)
        PERMB = sb("permb", [128, 128], BF16)
        ABF = sb("abf", [128, 1024], BF16)
        MFC = sb("mfc", [128, 256], F32)
        LGB = sb("lgb_sb", [128, DEPTH * 16], F32)
        VTT = sb("vtt_sb", [128, DEPTH * NV], F32)
        BGR = sb("bgr_sb", [128, DEPTH * 32], F32)
        SILC = sb("silc", [128, 8 * NS1], F32)
        MODF = sb("modf", [128, DEPTH * 24 * NS1], F32)
        AMOD = sb("amod", [128, DEPTH * 8 * NS1], F32)
        RC = sb("rc", [128, DEPTH * 48], F32)
        ONES = sb("ones", [128, 128], F32)
        EW = sb("ew", [128, NCH * 16], F32)
        THR = sb("thr", [128, NCH * 16], F32)
        ALB = sb("alb", [128, NCH * 16], F32)
        DM = sb("dm", [128, 128], F32)
        TMP = [sb("tmp%d" % i, [128, 512], F32) for i in range(4)]
        STB1 = sb("stb1", [128, 512], BF16)
        XO = sb("xo", [128, 1024], F32)
        SF = sb("sf", [128, 129], F32)
        SBK = sb("sbk", [128, 129], F32)
        SFB = [sb("sfb%d" % i, [128, 130], BF16) for i in range(2)]
        DG = sb("dg", [128, 10 * 128], BF16)
        SM = sb("small", [128, 112], F32)
        G8 = sb("g8", [8, 22 * NCH], F32)
        PS = [st.enter_context(nc.psum_tensor("ps%d" % i, [128, 512], F32)) for i in range(8)]

        def av(slot, c0, n, dt=BF16):
            esz = 2 if dt == BF16 else 4
            b0 = c0 * 2
            b1 = b0 + n * esz
            assert b1 <= SW * 2, (slot, c0, n)
            keys = [("A", slot, j) for j in range(b0 // 1024, (b1 - 1) // 1024 + 1)]
            base = slot * SW + c0
            if dt == BF16:
                ap = ARENA[:, base:base + n]
            else:
                assert base % 2 == 0
                ap = ARENA[:, base:base + 2 * n].bitcast(F32)
            return V(ap, keys)

        def av3(slot, c0, nb, w, dt=BF16, stride=None, sub=None):
            stride = stride or w
            v = av(slot, c0, nb * stride, dt)
            ap = v.ap.rearrange("p (a b) -> p a b", b=stride)
            if stride != w or sub is not None:
                lo, hi = (0, w) if sub is None else sub
                ap = ap[:, :, lo:hi]
            return V(ap, v.keys)

        def avk(slot0, nsl, c0, n):
            b0 = c0 * 2
            b1 = b0 + n * 2
            keys = [("A", slot0 + s, j) for s in range(nsl) for j in range(b0 // 1024, (b1 - 1) // 1024 + 1)]
            ap = ARENA[:, slot0 * SW:(slot0 + nsl) * SW].rearrange("p (k w) -> p k w", w=SW)[:, :, c0:c0 + n]
            return V(ap, keys)

        def pv(bank, c0=0, n=512, dt=F32, rows=128):
            if dt == F32:
                ap = PS[bank][0:rows, c0:c0 + n]
            else:
                ap = PS[bank][0:rows, :].bitcast(BF16)[:, c0:c0 + n]
            return V(ap, [("P", bank)])

        def pv3(bank, nb, w, dt=F32, stride=None, c0=0, rows=128, sub=None):
            stride = stride or w
            v = pv(bank, c0, nb * stride, dt, rows)
            ap = v.ap.rearrange("p (a b) -> p a b", b=stride)
            if stride != w or sub is not None:
                lo, hi = (0, w) if sub is None else sub
                ap = ap[:, :, lo:hi]
            return V(ap, v.keys)

        def tv(t, c0, n, key, rows=128):
            return V(t[0:rows, c0:c0 + n], [key])

        def tv3(t, c0, nb, w, key, rows=128):
            return V(t[0:rows, c0:c0 + nb * w].rearrange("p (a b) -> p a b", b=w), [key])

        def cstv(c0, n=128, rows=128):
            return V(CST[0:rows, c0:c0 + n], ["CST"])

        def vcol(l, c):
            return V(VTT[:, l * NV + c:l * NV + c + 1], ["VTT"])

        def bc3(v, shape):
            return V(v.ap.unsqueeze(2).to_broadcast(list(shape)), v.keys)

        def bc_mid(v, shape):
            return V(v.ap.unsqueeze(1).to_broadcast(list(shape)), v.keys)

        ring_i = [0]

        def ring_load(src_ap, ncols=128, width=None):
            r = ring_i[0]
            ring_i[0] = (r + 1) % NRING
            base = r * 1024
            dst = RING[:, base:base + 8 * ncols].rearrange("p (k n) -> p k n", n=ncols)
            v = V(dst, [("R", r)])
            p.dma(v, V(src_ap.rearrange("(k p) n -> p k n", p=128), ["wdram"]), eng="pool")
            return v, r

        def ring_slice(r, ncols, k, c0=0, n=None):
            n = ncols if n is None else n
            base = r * 1024 + k * ncols + c0
            return V(RING[:, base:base + n], [("R", r)])

        S_HT = 0
        S_YR = 8
        S_YM = 16
        S_W = 24

        def hT(k, t0, n):
            return av(S_HT + k, t0, n)

        IDBv = V(IDB[:], ["IDB"])
        IDFv = cstv(C_ID)

        p.dma(V(CST[:], ["CST"]), V(cst_d, ["d_cst"]))
        p.dma(V(ROPE[:].rearrange("p (a n) -> p a n", a=2), ["ROPE"]), V(rope_d.rearrange("a p n -> p a n"), ["d_rope"]))
        p.dma(V(LGB[:], ["LGB"]), V(lgb_d, ["d_lgb"]))
        p.dma(V(VTT[:].rearrange("p (l n) -> p l n", l=DEPTH), ["VTT"]), V(vtt.rearrange("l p n -> p l n"), ["d_vtt"]))
        p.dma(V(BGR[:].rearrange("p (l n) -> p l n", l=DEPTH), ["BGR"]), V(bgr.rearrange("l p n -> p l n"), ["d_bgr"]))
        p.dma(V(SILC[:], ["SILC"]), V(cT, ["d_cT"]))
        p.copy(IDBv, IDFv)
        p.copy(V(PERMB[:, 0:64], ["PERMB"]), cstv(C_ID + 64, 64))
        p.copy(V(PERMB[:, 64:128], ["PERMB"]), cstv(C_ID, 64))
        p.memset(V(ONES[:], ["ONES"]), 1.0)
        p.ts(V(MFC[:, 0:128], ["MFC"]), cstv(C_MF), CSCALE, ALU.mult)
        p.ts(V(MFC[:, 128:256], ["MFC"]), cstv(C_MB), CSCALE, ALU.mult)
        p.act(V(SILC[:], ["SILC"]), V(SILC[:], ["SILC"]), AF.Silu)

        for l in range(DEPTH):
            for kind, (cf, cb) in enumerate(((0, 1), (2, 3), (4, 4))):
                base = l * 48 + kind * 16
                p.ts(V(RC[:, base:base + 8], ["RC", l]), V(LGB[:, l * 16:l * 16 + 8], ["LGB"]),
                     V(CST[:, C_POS + cf:C_POS + cf + 1], ["CST"]), ALU.mult)
                p.ts(V(RC[:, base + 8:base + 16], ["RC", l]), V(LGB[:, l * 16 + 8:l * 16 + 16], ["LGB"]),
                     V(CST[:, C_POS + cb:C_POS + cb + 1], ["CST"]), ALU.mult)
            p.act(V(RC[:, l * 48:l * 48 + 48], ["RC", l]), V(RC[:, l * 48:l * 48 + 48], ["RC", l]), AF.Exp)
            p.ts(V(RC[:, l * 48:l * 48 + 16], ["RC", l]), V(RC[:, l * 48:l * 48 + 16], ["RC", l]), CSCALE, ALU.mult)
            for g in range(6):
                buf = (l * 6 + g) % 2
                s0 = (8 if buf == 0 else 19)
                stg_keys = [("A", s0 + s2, j) for s2 in range(8) for j in range(5)]
                stg = ARENA[:, s0 * SW:s0 * SW + 8192].bitcast(F32).rearrange("p (k n) -> p k n", n=512)
                p.dma(V(stg, stg_keys), V(w_ada[l][:, g * 512:(g + 1) * 512].rearrange("(k p) n -> p k n", p=128), ["wdram"]))
                for jj in range(4):
                    j = g * 4 + jj
                    for k in range(8):
                        p.matmul(pv(0, j * NS1, NS1), V(stg[:, k, jj * 128:(jj + 1) * 128], stg_keys),
                                 V(SILC[:, k * NS1:(k + 1) * NS1], ["SILC"]), start=(k == 0), stop=(k == 7))
            p.tt(tv3(MODF, l * 24 * NS1, 24, NS1, ("MODF", l)), pv3(0, 24, NS1),
                 bc3(V(VTT[:, l * NV + V_BADA:l * NV + V_BADA + 24], ["VTT"]), [128, 24, NS1]), ALU.add)
            p.ts(tv(AMOD, l * 8 * NS1, 8 * NS1, ("AMOD", l)), tv(MODF, (l * 24 + 8) * NS1, 8 * NS1, ("MODF", l)), 1.0, ALU.add)
            p.tt(tv3(AMOD, l * 8 * NS1, 8, NS1, ("AMOD", l)), tv3(AMOD, l * 8 * NS1, 8, NS1, ("AMOD", l)),
                 bc3(V(VTT[:, l * NV + V_NORMW:l * NV + V_NORMW + 8], ["VTT"]), [128, 8, NS1]), ALU.mult)

        def mod_sh(l, j, who):
            c = (l * 24 + j) * NS1 + who
            return V(MODF[:, c:c + 1], [("MODF", l)])

        def rsqrt_dve(y, x, t):
            yi = V(y.ap.bitcast(I32), y.keys)
            xi = V(x.ap.bitcast(I32), x.keys)
            p.add("dve", lambda e: e.tensor_single_scalar(yi.ap, xi.ap, 1, ALU.logical_shift_right), [x], [y], cost=0.2)
            p.add("dve", lambda e: e.tensor_scalar(yi.ap, yi.ap, -1.0, float(0x5f3759df), ALU.mult, ALU.add), [y], [y], cost=0.2)
            for _ in range(2):
                p.tt(t, y, y, ALU.mult)
                p.stt(t, t, -0.5, x, ALU.mult, ALU.mult)
                p.stt(y, t, 1.5, y, ALU.add, ALU.mult)

        def rmsnorm_stats(xv, junk, ss, rstd, tmpc):
            p.memset(ss, 0.0)
            p.act(junk, xv, AF.Square, accum_out=ss)
            p.ts(ss, ss, 1.0 / D, ALU.mult, EPS, ALU.add)
            rsqrt_dve(rstd, ss, tmpc)

        out_dmas = []

        WS_BASE = 8
        WS_N = 11
        pb_i = [0]

        PBANKS = (0, 1, 2, 7)

        def pbank():
            b = pb_i[0]
            pb_i[0] = (b + 1) % 4
            return PBANKS[b]

        def fkeys(a, b):
            ks = []
            for slot in range(a // SW, (b - 1) // SW + 1):
                lo = max(a, slot * SW) - slot * SW
                hi = min(b, (slot + 1) * SW) - slot * SW
                ks += [("A", slot, j) for j in range((lo * 2) // 1024, (hi * 2 - 1) // 1024 + 1)]
            return ks

        def interleave(ga, gb):
            ta = tb = 0.0
            a_done = b_done = False
            while not (a_done and b_done):
                if not a_done and (b_done or ta <= tb):
                    try:
                        ta += next(ga) or 1.0
                    except StopIteration:
                        a_done = True
                elif not b_done:
                    try:
                        tb += next(gb) or 1.0
                    except StopIteration:
                        b_done = True

        def drain(g):
            for _ in g:
                pass

        yst_i = [0]
        rope_i = [0]

        for s in range(NSEQ):
            for l in range(DEPTH):
                last = (l == DEPTH - 1)
                src = xin if l == 0 else xs
                wl = w_in[l]
                smc = lambda c, n=1: V(SM[:, c:c + n], [("SM", c)])

                SSQ = V(SM[:, 44:44 + NCH], [("SM", 44)])
                p.memset(SSQ, 0.0)
                for i in range(NCH):
                    xst = av(WS_BASE + (i % 3), 0, 1024, F32)
                    junk = av(WS_BASE + 3, 0, 1024, F32)
                    p.dma(xst, V(src[s, i * 128:(i + 1) * 128, :], [("xs", s, i)]))
                    p.act(junk, xst, AF.Square, accum_out=V(SM[:, 44 + i:45 + i], [("SM", 44)]))
                p.ts(SSQ, SSQ, 1.0 / D, ALU.mult, EPS, ALU.add)
                RSTD = V(SM[:, 64:64 + NCH], [("SM", 64)])
                rsqrt_dve(RSTD, SSQ, V(SM[:, 84:84 + NCH], [("SM", 84)]))
                for i in range(NCH):
                    who = NSEQ if i < NCC else s
                    xst = av(WS_BASE + (i % 3), 0, 1024, F32)
                    xn = av(WS_BASE + 4 + (i % 2), 0, 1024)
                    bnk = i % 2
                    p.dma(xst, V(src[s, i * 128:(i + 1) * 128, :], [("xs", s, i)]))
                    p.act(xn, xst, AF.Copy, scale=V(SM[:, 64 + i:65 + i], [("SM", 64)]))
                    for j in range(8):
                        p.transpose(pv(bnk, j * 128, 128, BF16), V(xn.ap[:, j * 128:(j + 1) * 128], xn.keys), IDBv)
                    t3 = av3(WS_BASE + 6 + (i % 2), 0, 8, 128, F32)
                    a_bc = V(AMOD[:, l * 8 * NS1:(l + 1) * 8 * NS1].rearrange("p (j s) -> p j s", s=NS1)[:, :, who:who + 1]
                             .to_broadcast([128, 8, 128]), [("AMOD", l)])
                    b_bc = V(MODF[:, l * 24 * NS1:(l * 24 + 8) * NS1].rearrange("p (j s) -> p j s", s=NS1)[:, :, who:who + 1]
                             .to_broadcast([128, 8, 128]), [("MODF", l)])
                    p.tt(t3, pv3(bnk, 8, 128, BF16), a_bc, ALU.mult)
                    p.tt(avk(S_HT, 8, i * 128, 128), t3, b_bc, ALU.add)

                def gate_prep():
                    wg, rg = ring_load(wl[:, GATE_OFF:GATE_OFF + 32], ncols=32)
                    GTs, NLFs, BCs, BKs, MUBs = [WS_BASE + WS_N + i for i in range(5)]
                    for i in range(NCH):
                        bank = 3 if i < 16 else 4
                        ii = i % 16
                        for k in range(8):
                            p.matmul(pv(bank, ii * 32, 32), hT(k, i * 128, 128), ring_slice(rg, 32, k),
                                     start=(k == 0), stop=(k == 7))
                    for b0 in range(0, NCH, 16):
                        nb = min(16, NCH - b0)
                        p.tt(av3(GTs, 2 * b0 * 32, nb, 32, F32), pv3(3 + b0 // 16, nb, 32),
                             bc_mid(V(BGR[:, l * 32:(l + 1) * 32], ["BGR"]), [128, nb, 32]), ALU.add)
                    for d in range(2):
                        order = fwd_order if d == 0 else bwd_order
                        gt3 = av3(GTs, 0, NCH, 32, F32)
                        fcols = V(gt3.ap[:, :, 8 + 16 * d:16 + 16 * d], gt3.keys)
                        icols = V(gt3.ap[:, :, 16 * d:16 * d + 8], gt3.keys)
                        nlf = av3(NLFs, 0, NCH, 8, F32)
                        p.act(nlf, fcols, AF.Exp, scale=-1.0)
                        p.ts(nlf, nlf, 1.0, ALU.add)
                        p.act(nlf, nlf, AF.Ln)
                        p.matmul(pv(5, 0, NCH * 8), cstv(C_MF if d == 0 else C_MB), av(NLFs, 0, NCH * 8, F32))
                        bc = av3(BCs, 0, NCH, 8, F32)
                        p.copy(bc, pv3(5, NCH, 8))
                        bk = av3(BKs, 0, NCH, 8, F32)
                        p.tt(bk, icols, bc, ALU.add)
                        g8 = lambda c0, n, key: V(G8[0:8, c0:c0 + n], [("G8", key)])
                        BMAX, GSUM, MP, MU = 0, NCH, 2 * NCH, 3 * NCH
                        for c0 in range(0, NCH, 4):
                            nb = min(4, NCH - c0)
                            for i in range(nb):
                                p.transpose(pv(6, i * 128, 128, rows=8), V(bk.ap[:, c0 + i, :], bk.keys), IDFv)
                            p.reduce(g8(BMAX + c0, nb, "bmax"), pv3(6, nb, 128, rows=8), ALU.max)
                        for n_ in range(NCH):
                            p.matmul(pv(3, n_, 1, rows=8), V(nlf.ap[:, n_, :], nlf.keys), V(ONES[:, 0:1], ["ONES"]))
                        p.copy(g8(GSUM, NCH, "gsum"), pv(3, 0, NCH, rows=8))
                        p.memset(g8(MP + order[0], 1, ("mp", order[0])), 0.0)
                        for si, n_ in enumerate(order):
                            p.tt(g8(MU + n_, 1, ("mu", n_)), g8(MP + n_, 1, ("mp", n_)), g8(BMAX + n_, 1, "bmax"), ALU.max)
                            if si + 1 < NCH:
                                n2 = order[si + 1]
                                p.tt(g8(MP + n2, 1, ("mp", n2)), g8(MU + n_, 1, ("mu", n_)), g8(GSUM + n_, 1, "gsum"), ALU.subtract)
                        allk = [("G8", ("mp", n_)) for n_ in range(NCH)] + [("G8", ("mu", n_)) for n_ in range(NCH)]
                        MPv = V(G8[0:8, MP:MP + NCH], allk)
                        MUv = V(G8[0:8, MU:MU + NCH], allk)
                        AL0 = 4 * NCH
                        alv = g8(AL0, NCH, "al")
                        p.tt(alv, MPv, MUv, ALU.subtract)
                        p.act(alv, alv, AF.Exp)
                        BD0 = 5 * NCH
                        bd = V(G8[0:8, BD0:BD0 + 16 * NCH].rearrange("p (w h n) -> p w h n", w=2, h=8), [("G8", "bd")])
                        i8 = V(CST[0:8, C_ID:C_ID + 8].unsqueeze(2).to_broadcast([8, 8, NCH]), ["CST"])
                        p.tt(V(bd.ap[:, 0], bd.keys), V(MUv.ap.unsqueeze(1).to_broadcast([8, 8, NCH]), MUv.keys), i8, ALU.mult)
                        p.tt(V(bd.ap[:, 1], bd.keys), V(alv.ap.unsqueeze(1).to_broadcast([8, 8, NCH]), alv.keys), i8, ALU.mult)
                        p.matmul(pv(4, 0, 16 * NCH), V(ONES[0:8, :], ["ONES"]), V(G8[0:8, BD0:BD0 + 16 * NCH], [("G8", "bd")]))
                        mub = av3(MUBs, 0, NCH, 8, F32)
                        psb = PS[4][:, 0:16 * NCH].rearrange("p (w h n) -> p w n h", w=2, h=8)
                        p.copy(mub, V(psb[:, 0], [("P", 4)]))
                        p.copy(V(ALB[:].rearrange("p (n r) -> p n r", r=16)[:, :, 8 * d:8 * d + 8], ["ALB"]), V(psb[:, 1], [("P", 4)]))
                        ewv = V(EW[:].rearrange("p (n r) -> p n r", r=16)[:, :, 8 * d:8 * d + 8], ["EW"])
                        thv = V(THR[:].rearrange("p (n r) -> p n r", r=16)[:, :, 8 * d:8 * d + 8], ["THR"])
                        p.tt(ewv, bk, mub, ALU.subtract)
                        p.act(ewv, ewv, AF.Exp)
                        p.tt(thv, bc, mub, ALU.subtract)
                        p.ts(thv, thv, -LNC, ALU.add)
                        p.act(thv, thv, AF.Exp)
                    yield 1.0

                def proj(wv_r, t0, n, bank):
                    for k in range(8):
                        p.matmul(pv(bank, 0, n), ring_slice(wv_r, 128, k), hT(k, t0, n), start=(k == 0), stop=(k == 7))
                    return pv(bank, 0, n)

                def transposes_bf16(src_fn, nb, bank):
                    for i in range(nb):
                        p.transpose(pv(bank, i * 128, 128, BF16), src_fn(i), IDBv)

                def head_norm_and_store(rt, nb, n, ys_idx, t0, nw_col, sz_slot, ws, tbank, sq_slot, sq_c0, hn_slot, hn_c0):
                    rt3 = V(rt.ap.rearrange("p (a b) -> p a b", b=128), rt.keys)
                    sq = av(sq_slot, sq_c0, n, F32)
                    s1 = smc(8, nb)
                    s2 = smc(12, nb)
                    m = smc(16, nb)
                    m2 = smc(20, nb)
                    p.reduce(s1, rt3, ALU.add)
                    p.tt(sq, rt, rt, ALU.mult)
                    p.reduce(s2, V(sq.ap.rearrange("p (a b) -> p a b", b=128), sq.keys), ALU.add)
                    p.stt(m2, s1, 1.0 / 16384, s1, ALU.mult, ALU.mult)
                    p.ts(s2, s2, 1.0 / 128, ALU.mult, EPS, ALU.add)
                    p.tt(s2, s2, m2, ALU.subtract)
                    p.ts(m, s1, 1.0 / 128, ALU.mult)
                    rsqrt_dve(m2, s2, smc(40, nb))
                    p.tt(rt3, rt3, bc3(m, [128, nb, 128]), ALU.subtract)
                    hn = av(hn_slot, hn_c0, n)
                    hn3 = V(hn.ap.rearrange("p (a b) -> p a b", b=128), hn.keys)
                    p.tt(hn3, rt3, bc3(m2, [128, nb, 128]), ALU.mult)
                    transposes_bf16(lambda i: V(hn.ap[:, i * 128:(i + 1) * 128], hn.keys), nb, tbank)
                    yb = yst_i[0]
                    yst_i[0] = 1 - yb
                    yst = av(30 + yb, 0, n)
                    p.stt(yst, pv(tbank, 0, n, BF16), nw_col, av(sz_slot, t0, n), ALU.mult, ALU.mult)
                    p.dma(V(ys[ys_idx, :, t0:t0 + n], [("ys", ys_idx, t0)]), yst)

                def P_ret(h, ws):
                    QT, KT, KZF, KZB, VTK, SZ, SBST = [ws + i for i in range(7)]
                    rcx = lambda kind, dd: V(RC[:, l * 48 + kind * 16 + dd * 8 + h:l * 48 + kind * 16 + dd * 8 + h + 1], [("RC", l)])
                    lgc = lambda dd: V(LGB[:, l * 16 + dd * 8 + h:l * 16 + dd * 8 + h + 1], ["LGB"])
                    p.act(V(TMP[0][:, 0:128], ["TMP0"]), cstv(C_QMK), AF.Exp, scale=lgc(0))
                    p.act(V(TMP[1][:, 0:128], ["TMP1"]), cstv(C_KMQ), AF.Exp, scale=lgc(1))
                    p.tt(V(TMP[0][:, 0:128], ["TMP0"]), V(TMP[0][:, 0:128], ["TMP0"]), V(MFC[:, 0:128], ["MFC"]), ALU.mult, eng="pool")
                    p.tt(V(TMP[1][:, 0:128], ["TMP1"]), V(TMP[1][:, 0:128], ["TMP1"]), V(MFC[:, 128:256], ["MFC"]), ALU.mult, eng="pool")
                    p.tt(V(DM[:], ["DM"]), V(TMP[0][:, 0:128], ["TMP0"]), V(TMP[1][:, 0:128], ["TMP1"]), ALU.add, eng="pool")
                    yield 0.5

                    def load_sw(c0):
                        r = ring_i[0]
                        ring_i[0] = (r + 1) % NRING
                        base = r * 1024
                        dst = RING[:, base:base + 1024].rearrange("p (k n) -> p k n", n=128)
                        for half in range(2):
                            p.dma(V(dst[:, :, half * 64:(half + 1) * 64], [("R", r)]),
                                  V(wl[:, c0 + (1 - half) * 64:c0 + (1 - half) * 64 + 64].rearrange("(k p) n -> p k n", p=128), ["wdram"]),
                                  eng="pool")
                        return r

                    _, rw_k = ring_load(wl[:, 1024 + h * 128:1024 + h * 128 + 128])
                    _, rv = ring_load(wl[:, 2048 + h * 128:2048 + h * 128 + 128])
                    _, rw_q = ring_load(wl[:, h * 128:h * 128 + 128])
                    _, rz = ring_load(wl[:, 3072 + h * 128:3072 + h * 128 + 128])

                    def qk(rw, rsw, bcol, bswcol, dst):
                        for (t0, n, isx) in blocks:
                            ps = proj(rw, t0, n, pbank())
                            if isx:
                                tx = t0 - LC
                                tb_ = rope_i[0]
                                rope_i[0] = 1 - tb_
                                abf = V(ABF[:, tb_ * 512:tb_ * 512 + n], [("ABF", tb_)])
                                t1 = V(TMP[2 * tb_][:, 0:n], ["TMP%d" % (2 * tb_)])
                                t2 = V(TMP[2 * tb_ + 1][:, 0:n], ["TMP%d" % (2 * tb_ + 1)])
                                p.act(abf, ps, AF.Identity, bias=vcol(l, bcol))
                                bk2 = pbank()
                                p.matmul(pv(bk2, 0, n), V(PERMB[:], ["PERMB"]), abf)
                                p.act(t2, pv(bk2, 0, n), AF.Copy)
                                p.tt(t1, abf, V(ROPE[:, tx:tx + n], ["ROPE"]), ALU.mult, eng="pool")
                                p.tt(t2, t2, V(ROPE[:, LX + tx:LX + tx + n], ["ROPE"]), ALU.mult, eng="pool")
                                p.tt(av(dst, t0, n), t1, t2, ALU.add, eng="pool")
                            else:
                                p.act(av(dst, t0, n), ps, AF.Identity, bias=vcol(l, bcol))
                            yield (4.2 if isx else 2.1) * n / 512

                    yield from qk(rw_k, None, V_BIN + 8 + h, None, KT)
                    for (t0, n, isx) in blocks:
                        nb = n // 128
                        bk_ = pbank()
                        transposes_bf16(lambda i: av(KT, t0 + i * 128, 128), nb, bk_)
                        p.act(av(KZF, t0, n), pv(bk_, 0, n, BF16), AF.Copy, scale=rcx(1, 0))
                        p.act(av(KZB, t0, n), pv(bk_, 0, n, BF16), AF.Copy, scale=rcx(1, 1))
                        yield 1.2 * n / 512
                    for (t0, n, isx) in blocks:
                        nb = n // 128
                        ps = proj(rv, t0, n, pbank())
                        vt = V(STB1[:, 0:n], ["STB1"])
                        p.act(vt, ps, AF.Identity, bias=vcol(l, V_BIN + 16 + h))
                        bk_ = pbank()
                        transposes_bf16(lambda i: V(STB1[:, i * 128:(i + 1) * 128], ["STB1"]), nb, bk_)
                        p.copy(av(VTK, t0, n), pv(bk_, 0, n, BF16), eng="act")
                        yield 2.8 * n / 512
                    yield from qk(rw_q, None, V_BIN + h, None, QT)
                    for (t0, n, isx) in blocks:
                        if not (last and not isx):
                            ps = proj(rz, t0, n, pbank())
                            p.act(av(SZ, t0, n), ps, AF.Silu, bias=vcol(l, V_BIN + 24 + h))
                            yield 2.1 * n / 512

                def chain_groups():
                    gf, gb = [], []
                    for (t0, n, isx) in blocks:
                        gf.append(list(range(t0 // 128, t0 // 128 + n // 128)))
                    cb = [g for g, b in zip(gf, blocks) if not b[2]]
                    xb = [g for g, b in zip(gf, blocks) if b[2]]
                    for g in reversed(cb):
                        gb.append(list(reversed(g)))
                    for g in reversed(xb):
                        gb.append(list(reversed(g)))
                    gf2 = [g for g, b in zip(gf, blocks) if not b[2]] + xb
                    return gf2, gb

                def C_ret(h, ws):
                    QT, KT, KZF, KZB, VTK, SZ, SBST, SFST = [ws + i for i in range(8)]
                    rcx = lambda kind, dd: V(RC[:, l * 48 + kind * 16 + dd * 8 + h:l * 48 + kind * 16 + dd * 8 + h + 1], [("RC", l)])
                    sfv = V(SF[:, 0:128], ["SF"])
                    sbv = V(SBK[:, 0:128], ["SBK"])
                    p.memset(sfv, 0.0)
                    p.memset(sbv, 0.0)
                    gf, gb = chain_groups()
                    kb = [3, 5]
                    for gi in range(len(gf)):
                        bf_, bb_ = kb[gi % 2], kb[gi % 2] + 1
                        for i, c in enumerate(gf[gi]):
                            p.matmul(pv(bf_, i * 128, 128), av(KZF, c * 128, 128), av(VTK, c * 128, 128))
                        for i, c in enumerate(gb[gi]):
                            p.matmul(pv(bb_, i * 128, 128), av(KZB, c * 128, 128), av(VTK, c * 128, 128))
                        for i in range(max(len(gf[gi]), len(gb[gi]))):
                            if i < len(gf[gi]):
                                p.copy(av(SFST, gf[gi][i] * 128, 128), sfv)
                            if i < len(gb[gi]):
                                p.copy(av(SBST, gb[gi][i] * 128, 128), sbv)
                            lastf = (gi == len(gf) - 1 and i == len(gf[gi]) - 1)
                            if i < len(gf[gi]) and not lastf:
                                p.stt(sfv, sfv, rcx(2, 0), pv(bf_, i * 128, 128), ALU.mult, ALU.add)
                            if i < len(gb[gi]) and not lastf:
                                p.stt(sbv, sbv, rcx(2, 1), pv(bb_, i * 128, 128), ALU.mult, ALU.add)
                        yield 1.3 * len(gf[gi])
                    for (t0, n, isx) in blocks:
                        nb = n // 128
                        c0 = t0 // 128
                        if last and not isx:
                            continue
                        for i in range(nb):
                            p.matmul(pv(3, i * 128, 128), av(KT, t0 + i * 128, 128), av(QT, t0 + i * 128, 128))
                        stv = av(ws + 10, 0, n)
                        p.tt(V(stv.ap.rearrange("p (a b) -> p a b", b=128), stv.keys), pv3(3, nb, 128),
                             bc_mid(V(DM[:], ["DM"]), [128, nb, 128]), ALU.mult)
                        for i in range(nb):
                            c = c0 + i
                            p.matmul(pv(5, i * 128, 128), av(QT, c * 128, 128), av(SFST, c * 128, 128))
                            p.matmul(pv(6, i * 128, 128), av(QT, c * 128, 128), av(SBST, c * 128, 128))
                        yield 2.0 * n / 512
                        for i in range(nb):
                            c = c0 + i
                            p.matmul(pv(4, i * 128, 128), V(stv.ap[:, i * 128:(i + 1) * 128], stv.keys), av(VTK, c * 128, 128))
                        rt = av(ws + 8, 0, n, F32)
                        p.ts(rt, pv(5, 0, n), rcx(0, 0), ALU.mult)
                        p.stt(rt, pv(6, 0, n), rcx(0, 1), rt, ALU.mult, ALU.add)
                        p.tt(rt, pv(4, 0, n), rt, ALU.add)
                        yield 2.5 * n / 512
                        head_norm_and_store(rt, nb, n, h, t0, vcol(l, V_RNW + h), SZ, ws, 3, ws + 9, 0, ws + 10, 512)
                        yield 8.0 * n / 512

                def P_ml(h, ws):
                    UQ, UK, MQT, MKT, MKTK, VPF, VPB, OT = [ws + i for i in range(8)]
                    SZm = UQ
                    _, rw_k = ring_load(wl[:, 5120 + h * 128:5120 + h * 128 + 128])
                    _, rw_q = ring_load(wl[:, 4096 + h * 128:4096 + h * 128 + 128])
                    _, rv = ring_load(wl[:, 6144 + h * 128:6144 + h * 128 + 128])
                    _, ro = ring_load(wl[:, 7168 + h * 128:7168 + h * 128 + 128])
                    _, rz = ring_load(wl[:, 8192 + h * 128:8192 + h * 128 + 128])
                    for qk_ in range(2):
                        for j in range(5):
                            p.act(V(DG[:, (qk_ * 5 + j) * 128:(qk_ * 5 + j + 1) * 128], [("DG", qk_)]), IDFv, AF.Copy,
                                  scale=vcol(l, V_CONVW + j * 16 + qk_ * 8 + h))
                    pcol = lambda t0, isx: t0 + (4 if isx else 2)
                    for ub in (UQ, UK):
                        p.memset(av(ub, 0, 2), 0.0, eng="pool")
                        p.memset(av(ub, LC + 2, 2), 0.0, eng="pool")
                        p.memset(av(ub, T + 4, 2), 0.0, eng="pool")
                    yield 1.0
                    for (rw, bcol, ub) in ((rw_k, V_BIN + 40 + h, UK), (rw_q, V_BIN + 32 + h, UQ)):
                        for (t0, n, isx) in blocks:
                            ps = proj(rw, t0, n, pbank())
                            p.act(av(ub, pcol(t0, isx), n), ps, AF.Identity, bias=vcol(l, bcol))
                            yield 2.1 * n / 512
                    for (qk_, ub, dst) in ((1, UK, MKT), (0, UQ, MQT)):
                        for (t0, n, isx) in blocks:
                            bk_ = pbank()
                            for j in range(5):
                                p.matmul(pv(bk_, 0, n), V(DG[:, (qk_ * 5 + j) * 128:(qk_ * 5 + j + 1) * 128], [("DG", qk_)]),
                                         av(ub, pcol(t0, isx) - 2 + j, n), start=(j == 0), stop=(j == 4))
                            p.act(av(dst, t0, n), pv(bk_, 0, n), AF.Silu, bias=vcol(l, V_CONVB + qk_ * 8 + h))
                            yield 1.5 * n / 512
                    for (t0, n, isx) in blocks:
                        nb = n // 128
                        bk_ = pbank()
                        transposes_bf16(lambda i: av(MKT, t0 + i * 128, 128), nb, bk_)
                        p.copy(av(MKTK, t0, n), pv(bk_, 0, n, BF16), eng="act")
                        yield 1.0 * n / 512
                    ew3 = EW[:].rearrange("p (n r) -> p n r", r=16)
                    for (t0, n, isx) in blocks:
                        nb = n // 128
                        c0 = t0 // 128
                        ps = proj(rv, t0, n, pbank())
                        vt = V(STB1[:, 0:n], ["STB1"])
                        p.act(vt, ps, AF.Identity, bias=vcol(l, V_BIN + 48 + h))
                        bk_ = pbank()
                        transposes_bf16(lambda i: V(STB1[:, i * 128:(i + 1) * 128], ["STB1"]), nb, bk_)
                        for dd, slot in ((0, VPF), (1, VPB)):
                            ewc = V(ew3[:, c0:c0 + nb, dd * 8 + h:dd * 8 + h + 1], ["EW"])
                            for i in range(nb):
                                p.act(av(slot, (c0 + i) * 129, 128), pv(bk_, i * 128, 128, BF16), AF.Copy,
                                      scale=V(ew3[:, c0 + i, dd * 8 + h:dd * 8 + h + 1], ["EW"]))
                            p.copy(av3(slot, c0 * 129, nb, 129, stride=129, sub=(128, 129)), ewc, eng="pool")
                        yield 4.0 * n / 512
                    for (t0, n, isx) in blocks:
                        if not (last and not isx):
                            ps = proj(ro, t0, n, pbank())
                            p.act(av(OT, t0, n), ps, AF.Sigmoid, bias=vcol(l, V_BIN + 56 + h))
                            yield 2.1 * n / 512
                    for (t0, n, isx) in blocks:
                        if not (last and not isx):
                            ps = proj(rz, t0, n, pbank())
                            p.act(av(SZm, t0, n), ps, AF.Silu, bias=vcol(l, V_BIN + 64 + h))
                            yield 2.1 * n / 512

                def C_ml(h, ws):
                    UQ, UK, MQT, MKT, MKTK, VPF, VPB, OT = [ws + i for i in range(8)]
                    SZm, CBST, CFST = UQ, UK, ws + 10
                    al3 = ALB[:].rearrange("p (n r) -> p n r", r=16)
                    alc = lambda n_, dd: V(al3[:, n_, dd * 8 + h:dd * 8 + h + 1], ["ALB"])
                    cfv = V(SF[:, 0:129], ["SF"])
                    cbv = V(SBK[:, 0:129], ["SBK"])
                    p.memset(cfv, 0.0)
                    p.memset(cbv, 0.0)
                    gf, gb = chain_groups()
                    for gi in range(len(gf)):
                        for i, c in enumerate(gf[gi]):
                            p.matmul(pv(3 + i // 2, (i % 2) * 256, 129), av(MKTK, c * 128, 128), av(VPF, c * 129, 129))
                        for i, c in enumerate(gb[gi]):
                            p.matmul(pv(5 + i // 2, (i % 2) * 256, 129), av(MKTK, c * 128, 128), av(VPB, c * 129, 129))
                        for i in range(max(len(gf[gi]), len(gb[gi]))):
                            lastf = (gi == len(gf) - 1 and i == len(gf[gi]) - 1)
                            if i < len(gf[gi]):
                                c = gf[gi][i]
                                p.ts(av(CFST, c * 129, 129), cfv, alc(c, 0), ALU.mult)
                            if i < len(gb[gi]):
                                c = gb[gi][i]
                                p.ts(av(CBST, c * 129, 129), cbv, alc(c, 1), ALU.mult)
                            if i < len(gf[gi]) and not lastf:
                                c = gf[gi][i]
                                p.stt(cfv, cfv, alc(c, 0), pv(3 + i // 2, (i % 2) * 256, 129), ALU.mult, ALU.add)
                            if i < len(gb[gi]) and not lastf:
                                c = gb[gi][i]
                                p.stt(cbv, cbv, alc(c, 1), pv(5 + i // 2, (i % 2) * 256, 129), ALU.mult, ALU.add)
                        yield 1.4 * len(gf[gi])
                    th3 = THR[:].rearrange("p (n r) -> p n r", r=16)
                    for (t0, n, isx) in blocks:
                        nb = n // 128
                        c0 = t0 // 128
                        if last and not isx:
                            continue
                        for i in range(nb):
                            p.matmul(pv(3, i * 128, 128), av(MKT, t0 + i * 128, 128), av(MQT, t0 + i * 128, 128))
                        stb_ = av(ws + 9, 0, n)
                        stf = av(ws + 9, 512, n)
                        p.tt(V(stf.ap.rearrange("p (a b) -> p a b", b=128), stf.keys), pv3(3, nb, 128),
                             bc_mid(cstv(C_MF), [128, nb, 128]), ALU.mult)
                        p.tt(V(stb_.ap.rearrange("p (a b) -> p a b", b=128), stb_.keys), pv3(3, nb, 128),
                             bc_mid(cstv(C_MB), [128, nb, 128]), ALU.mult)
                        yield 2.5 * n / 512
                        ot = av(ws + 8, 0, n, F32)
                        ot3 = V(ot.ap.rearrange("p (a b) -> p a b", b=128), ot.keys)
                        t1 = av(ws + 8, 1024, n, F32)
                        t13 = V(t1.ap.rearrange("p (a b) -> p a b", b=128), t1.keys)

                        def one_dir(dd, stv_, vp_, cst_, dstv):
                            for i in range(nb):
                                c = c0 + i
                                b_, c_ = 4 + i // 2, (i % 2) * 256
                                p.matmul(pv(b_, c_, 129), V(stv_.ap[:, i * 128:(i + 1) * 128], stv_.keys), av(vp_, c * 129, 129), start=True, stop=False)
                                p.matmul(pv(b_, c_, 129), av(MQT, c * 128, 128), av(cst_, c * 129, 129), start=False, stop=True)
                            dn = smc(24 + dd * 4, nb)
                            ab = smc(32 + dd * 4, nb)
                            for bq in range((nb + 1) // 2):
                                nn = min(2, nb - 2 * bq)
                                num = PS[4 + bq][:, 0:nn * 256].rearrange("p (a b) -> p a b", b=256)
                                p.copy(V(SM[:, 24 + dd * 4 + 2 * bq:24 + dd * 4 + 2 * bq + nn], [("SM", 24 + dd * 4)]),
                                       V(num[:, :, 128], [("P", 4 + bq)]))
                            p.stt(ab, dn, -1.0, dn, ALU.mult, ALU.max)
                            p.tt(ab, ab, V(th3[:, c0:c0 + nb, dd * 8 + h], ["THR"]), ALU.max)
                            p.recip(ab, ab)
                            if dd == 0:
                                for bq in range((nb + 1) // 2):
                                    nn = min(2, nb - 2 * bq)
                                    num = PS[4 + bq][:, 0:nn * 256].rearrange("p (a b) -> p a b", b=256)
                                    p.tt(V(dstv.ap[:, 2 * bq:2 * bq + nn, :], dstv.keys), V(num[:, :, 0:128], [("P", 4 + bq)]),
                                         V(SM[:, 32 + dd * 4 + 2 * bq:32 + dd * 4 + 2 * bq + nn].unsqueeze(2).to_broadcast([128, nn, 128]),
                                           [("SM", 32 + dd * 4)]), ALU.mult)
                            else:
                                for i in range(nb):
                                    p.stt(V(dstv.ap[:, i, :], dstv.keys), pv(4 + i // 2, (i % 2) * 256, 128),
                                          V(SM[:, 32 + dd * 4 + i:33 + dd * 4 + i], [("SM", 32 + dd * 4)]),
                                          V(dstv.ap[:, i, :], dstv.keys), ALU.mult, ALU.add)

                        one_dir(0, stf, VPF, CFST, ot3)
                        yield 3.5 * n / 512
                        one_dir(1, stb_, VPB, CBST, ot3)
                        yield 3.5 * n / 512
                        transposes_bf16(lambda i: av(OT, t0 + i * 128, 128), nb, 6)
                        p.tt(ot, ot, pv(6, 0, n, BF16), ALU.mult)
                        yield 2.5 * n / 512
                        head_norm_and_store(ot, nb, n, 8 + h, t0, vcol(l, V_MNW + h), SZm, ws, 3, ws + 8, 1024, ws + 9, 1024)
                        yield 8.0 * n / 512

                units = []
                for h in range(8):
                    units.append((P_ret, C_ret, h))
                    units.append((P_ml, C_ml, h))
                wsof = lambda u: WS_BASE + WS_N * (u % 2)
                MW0 = 8 * SW

                def mw(i):
                    a = MW0 + i * 8192
                    return ARENA[:, a:a + 8192].rearrange("p (k n) -> p k n", n=1024), fkeys(a, a + 8192)

                MW = []

                def merge_weight_loads(i0, i1):
                    wsrcs = [w_ro[l], w_mo[l], wl[:, 9248:10272], wl[:, 10272:11296], w_out[l]]
                    for i in range(i0, i1):
                        wsrc = wsrcs[i]
                        ap_, ks_ = mw(i)
                        p.dma(V(ap_, ks_), V(wsrc.rearrange("(k p) n -> p k n", p=128), ["wdram"]), eng="pool")
                        MW.append((ap_, ks_))
                        yield

                Pstream = p.collect(units[0][0](units[0][2], wsof(0)))
                Cstream = p.collect(gate_prep())
                for u in range(len(units)):
                    if u + 1 < len(units):
                        Pstream += p.collect(units[u + 1][0](units[u + 1][2], wsof(u + 1)))
                    Cstream += p.collect(units[u][1](units[u][2], wsof(u)))
                Pstream += p.collect(merge_weight_loads(0, 5))
                p.schedule([Pstream, Cstream])

                if dbg and s == NSEQ - 1 and last:
                    pass

                YS0 = 26 * SW
                YT0 = 30 * SW
                GBX = V(RING[:, 0:2048].bitcast(F32), [("R", 0), ("R", 1)])
                GBC = V(RING[:, 2048:4096].bitcast(F32), [("R", 2), ("R", 3)])
                XS_ = V(RING[:, 4096:6144].bitcast(F32), [("R", 4), ("R", 5)])
                XO_ = V(XO[:], ["XO"])
                if last:
                    p.dma(GBC, V(fnw_d, ["d_fnw"]))
                for (gb, who) in ((GBX, s), (GBC, NSEQ)):
                    if last and who == NSEQ:
                        continue
                    for j in range(8):
                        c_ = (l * 24 + 16 + j) * NS1 + who
                        gt_ = V(TMP[0][:, 0:128], ["TMP0"])
                        p.ts(gt_, IDFv, V(MODF[:, c_:c_ + 1], [("MODF", l)]), ALU.mult)
                        p.matmul(pv(j // 4, (j % 4) * 128, 128), V(ONES[:], ["ONES"]), gt_)
                    for half in range(2):
                        p.copy(V(gb.ap[:, half * 512:(half + 1) * 512], gb.keys), pv(half, 0, 512), eng="act")
                mblocks = [b for b in blocks if not (last and not b[2])]
                hbs = []
                for (t0, n, isx) in mblocks:
                    for o in range(0, n, 256):
                        hbs.append((t0 + o, min(256, n - o), isx))

                def ysb_view(bi, n):
                    a0 = YS0 + bi * 4096
                    return ARENA[:, a0:a0 + 4096].rearrange("p (k n) -> p k n", n=256)[:, :, 0:n], fkeys(a0, a0 + 4096)

                def ytb_view(bi):
                    a0 = YT0 + bi * 2048
                    return ARENA[:, a0:a0 + 2048].rearrange("p (k n) -> p k n", n=256), fkeys(a0, a0 + 2048)

                def J_stage(hb):
                    t0, n, isx = hbs[hb]
                    bi = hb % 2
                    ysb_ap, ysb_k = ysb_view(bi, n)
                    ytb_ap, ytb_k = ytb_view(bi)
                    p.dma(V(ysb_ap, ysb_k), V(ys[:, :, t0:t0 + n].rearrange("k p t -> p k t"), [("ys", k_, (t0 // 512) * 512) for k_ in range(16)]))
                    for j in range(8):
                        c0 = (j % 2) * 256
                        for k in range(8):
                            p.matmul(pv(0, c0, n), V(MW[0][0][:, k, j * 128:(j + 1) * 128], MW[0][1]), V(ysb_ap[:, k, :], ysb_k), start=(k == 0), stop=(k == 7))
                        for k in range(8):
                            p.matmul(pv(1, c0, n), V(MW[1][0][:, k, j * 128:(j + 1) * 128], MW[1][1]), V(ysb_ap[:, 8 + k, :], ysb_k), start=(k == 0), stop=(k == 7))
                        for k in range(8):
                            p.matmul(pv(2, c0, n), V(MW[2][0][:, k, j * 128:(j + 1) * 128], MW[2][1]), hT(k, t0, n), start=(k == 0), stop=(k == 7))
                        for k in range(8):
                            p.matmul(pv(3, c0, n), V(MW[3][0][:, k, j * 128:(j + 1) * 128], MW[3][1]), hT(k, t0, n), start=(k == 0), stop=(k == 7))
                        sg1 = V(TMP[0][:, c0:c0 + n], [("TMP0", j % 2)])
                        sg2 = V(TMP[1][:, c0:c0 + n], [("TMP1", j % 2)])
                        p.act(sg1, pv(2, c0, n), AF.Sigmoid, bias=vcol(l, V_BG + j))
                        p.act(sg2, pv(3, c0, n), AF.Sigmoid, bias=vcol(l, V_BG + 8 + j))
                        p.tt(sg1, pv(0, c0, n), sg1, ALU.mult)
                        p.tt(sg2, pv(1, c0, n), sg2, ALU.mult)
                        p.tt(V(ytb_ap[:, j, 0:n], ytb_k), sg1, sg2, ALU.add)
                        yield

                otile = [0]

                def O_stage(hb):
                    t0, n, isx = hbs[hb]
                    bi = hb % 2
                    ytb_ap, ytb_k = ytb_view(bi)
                    for ti in range(n // 128):
                        i = t0 // 128 + ti
                        gb = GBX if isx else GBC
                        par = otile[0] % 2
                        otile[0] += 1
                        if par == 0:
                            xs_h = [V(XS_.ap[:, hf * 512:(hf + 1) * 512], [("R", 4 + hf)]) for hf in range(2)]
                        else:
                            xs_h = [V(TMP[2 + hf][:, :], ["TMP%d" % (2 + hf)]) for hf in range(2)]
                        for hf in range(2):
                            p.dma(xs_h[hf], V(src[s, i * 128:(i + 1) * 128, hf * 512:(hf + 1) * 512], [("xs", s, i)]))
                        for half in range(2):
                            bk_ = 4 + par * 2 + half
                            for k in range(8):
                                p.matmul(pv(bk_, 0, 512), V(ytb_ap[:, k, ti * 128:(ti + 1) * 128], ytb_k),
                                         V(MW[4][0][:, k, half * 512:(half + 1) * 512], MW[4][1]), start=(k == 0), stop=(k == 7))
                            xo_h = V(XO_.ap[:, half * 512:(half + 1) * 512], XO_.keys)
                            p.tt(xo_h, pv(bk_, 0, 512), V(gb.ap[:, half * 512:(half + 1) * 512], gb.keys), ALU.mult)
                            p.tt(xo_h, xo_h, xs_h[half], ALU.add)
                        if not last:
                            p.dma(V(xs[s, i * 128:(i + 1) * 128, :], [("xs", s, i)]), XO_)
                        else:
                            junk_ = V(STB1[:, :].bitcast(F32), ["STB1"]) if False else xs_h[0]
                            p.memset(smc(3), 0.0)
                            p.memset(smc(4), 0.0)
                            for hf in range(2):
                                p.act(xs_h[hf], V(XO_.ap[:, hf * 512:(hf + 1) * 512], XO_.keys), AF.Square, accum_out=smc(3 + hf))
                            p.tt(smc(0), smc(3), smc(4), ALU.add)
                            p.ts(smc(0), smc(0), 1.0 / D, ALU.mult, EPS, ALU.add)
                            rsqrt_dve(smc(1), smc(0), smc(2))
                            p.act(XO_, XO_, AF.Copy, scale=smc(1))
                            p.tt(XO_, XO_, GBC, ALU.mult)
                            p.dma(V(out[s, (i - NCC) * 128:(i - NCC + 1) * 128, :], [("out", s, i)]), XO_)
                        yield

                Js, Os = [], []
                nh = len(hbs)
                order = []
                for hb in range(nh):
                    order.append(("J", hb))
                    if hb >= 1:
                        order.append(("O", hb - 1))
                order.append(("O", nh - 1))
                for kind, hb in order:
                    if kind == "J":
                        Js += p.collect(J_stage(hb))
                    else:
                        Os += p.collect(O_stage(hb))
                p.schedule([Js, Os])

        p.emit()
    return nc, len(p.ops)


def _host_consts(LX):
    k = np.arange(128, dtype=np.float32)[:, None]
    q = np.arange(128, dtype=np.float32)[None, :]
    cst = np.zeros((128, NCST), np.float32)
    cst[:, C_ID:C_ID + 128] = np.eye(128, dtype=np.float32)
    cst[:, C_MF:C_MF + 128] = (k <= q)
    cst[:, C_MB:C_MB + 128] = (k >= q)
    cst[:, C_QMK:C_QMK + 128] = np.maximum(q - k, 0)
    cst[:, C_KMQ:C_KMQ + 128] = np.maximum(k - q, 0)
    pidx = np.arange(128, dtype=np.float32)
    cst[:, C_POS + 0] = pidx + 1
    cst[:, C_POS + 1] = 128 - pidx
    cst[:, C_POS + 2] = 127 - pidx
    cst[:, C_POS + 3] = pidx
    cst[:, C_POS + 4] = 128.0
    GRID_W = 64
    rows_n = LX // GRID_W
    rows = np.repeat(np.arange(rows_n, dtype=np.float32), GRID_W)
    cols = np.tile(np.arange(GRID_W, dtype=np.float32), rows_n)
    nf = 32
    freqs = (np.float32(10000.0) ** (-np.arange(nf, dtype=np.float32) / np.float32(nf))).astype(np.float32)
    ang = np.concatenate([rows[:, None] * freqs, cols[:, None] * freqs], axis=-1).astype(np.float32)
    cos = np.cos(ang).astype(np.float32).T
    sin = np.sin(ang).astype(np.float32).T
    rope = np.zeros((2, 128, LX), np.float32)
    rope[0, :64] = cos
    rope[0, 64:] = cos
    rope[1, :64] = -sin
    rope[1, 64:] = sin
    return cst, rope


def _host_tables(norm_w, b_ada, b_in, conv_w, conv_b, ret_norm_w, ml_norm_w, ret_log_gamma):
    depth = norm_w.shape[0]
    vtt = np.zeros((depth, 128, NV), np.float32)
    bgr = np.zeros((depth, 128, 32), np.float32)
    for l in range(depth):
        vtt[l, :, V_BADA:V_BADA + 24] = b_ada[l].reshape(24, 128).T
        vtt[l, :, V_NORMW:V_NORMW + 8] = norm_w[l].reshape(8, 128).T
        vtt[l, :, V_BIN:V_BIN + 72] = b_in[l][:GATE_OFF].reshape(72, 128).T
        vtt[l, :, V_BG:V_BG + 16] = b_in[l][GATE_OFF + 32:].reshape(16, 128).T
        vtt[l, :, V_CONVB:V_CONVB + 16] = conv_b[l].reshape(16, 128).T
        vtt[l, :, V_CONVW:V_CONVW + 80] = conv_w[l].reshape(80, 128).T
        vtt[l, :, V_RNW:V_RNW + 8] = ret_norm_w[l].reshape(8, 128).T
        vtt[l, :, V_MNW:V_MNW + 8] = ml_norm_w[l].reshape(8, 128).T
        bq = b_in[l][:2048].reshape(16, 128)
        bsw = np.concatenate([bq[:, 64:], bq[:, :64]], axis=1)
        vtt[l, :, V_BSW:V_BSW + 16] = bsw.T
        bgr[l] = np.tile(b_in[l][GATE_OFF:GATE_OFF + 32][None, :], (128, 1))
    lgb = np.tile(ret_log_gamma.reshape(1, depth * 16), (128, 1)).astype(np.float32)
    return vtt, bgr, lgb


_CACHE = {}


def run_config(inputs, n_cores, NSEQ, LC, LX, DEPTH):
    f = lambda a: np.ascontiguousarray(np.asarray(a, dtype=np.float32))
    x, c, ctx, c_ctx = f(inputs["x"]), f(inputs["c"]), f(inputs["ctx"]), f(inputs["c_ctx"])
    key = (NSEQ, LC, LX, DEPTH)
    if key not in _CACHE:
        _CACHE[key] = build_program(NSEQ, LC, LX, DEPTH)
    nc, nops = _CACHE[key]
    cst, rope = _host_consts(LX)
    vtt, bgr, lgb = _host_tables(f(inputs["norm_w"]), f(inputs["b_ada"]), f(inputs["b_in"]), f(inputs["conv_w"]),
                                 f(inputs["conv_b"]), f(inputs["ret_norm_w"]), f(inputs["ml_norm_w"]),
                                 f(inputs["ret_log_gamma"]))
    fnw = np.ascontiguousarray(np.tile(f(inputs["final_norm_w"])[None, :], (128, 1)))
    shared = {"w_ada": f(inputs["w_ada"]), "w_in": f(inputs["w_in"]), "w_ret_o": f(inputs["w_ret_o"]),
              "w_ml_o": f(inputs["w_ml_o"]), "w_out": f(inputs["w_out"]), "vtt": vtt, "bgr": bgr, "lgb": lgb,
              "cst": cst, "rope": rope, "fnw": fnw}
    in_maps = []
    for ci in range(n_cores):
        sl = slice(ci * NSEQ, (ci + 1) * NSEQ)
        xin = np.ascontiguousarray(np.concatenate([ctx[sl], x[sl]], axis=1))
        cc = np.concatenate([c[sl], c_ctx[None, :]], axis=0)
        cT = np.ascontiguousarray(cc.reshape(NSEQ + 1, 8, 128).transpose(2, 1, 0).reshape(128, 8 * (NSEQ + 1)))
        m = dict(shared)
        m["xin"] = xin
        m["cT"] = cT
        in_maps.append(m)
    res = run_bass_kernel_spmd(nc, in_maps, core_ids=list(range(n_cores)))
    return np.concatenate([r["out"] for r in res.results], axis=0)


def kernel(**inputs):
    return run_config(inputs, 8, 4, 256, 2048, 2)
```
